# Optimizing a Trainium2 kernel written in Bass

```python
import math
import jax, jax.numpy as jnp
from jax import lax
import numpy as np

D_MODEL = 1024
BATCH = 16
SEQ = 2048
DEPTH = 4

D_MIX = 2 * D_MODEL
ATTN_WIDTH = D_MIX // 2
V_HEAD_DIM = 128
QK_HEAD_DIM = V_HEAD_DIM // 2
ATTN_HEADS = ATTN_WIDTH // V_HEAD_DIM
Q_WIDTH = ATTN_HEADS * 2 * QK_HEAD_DIM
K_WIDTH = ATTN_HEADS * 2 * QK_HEAD_DIM
V_WIDTH = ATTN_HEADS * V_HEAD_DIM
LRU_WIDTH = D_MIX - ATTN_WIDTH
LRU_BLOCKS = 8
LRU_BLOCK = LRU_WIDTH // LRU_BLOCKS
LRU_C = 8.0
CONV_WIDTH = 4
Q_BLOCK = 128
NORM_EPS = 1e-6
IN_SPLITS = (Q_WIDTH, K_WIDTH, V_WIDTH, ATTN_WIDTH, LRU_WIDTH, LRU_WIDTH)
D_IN_PROJ = sum(IN_SPLITS)
SPLIT_POINTS = tuple(int(v) for v in np.cumsum(IN_SPLITS)[:-1])

kernel_name = "hymba_diffattn_rglru_sandwich"


def rmsnorm(x, g):
    xf = x.astype(jnp.float32)
    y = xf * lax.rsqrt(jnp.mean(xf * xf, axis=-1, keepdims=True) + NORM_EPS)
    return (y * g.astype(jnp.float32)).astype(x.dtype)


def lambda_init_for(layer_idx):
    return 0.8 - 0.6 * math.exp(-0.3 * layer_idx)


def diff_attention(q, k, v, lam):
    S = q.shape[1]
    scale = QK_HEAD_DIM ** -0.5
    outs = []
    for i in range(S // Q_BLOCK):
        q0 = i * Q_BLOCK
        end = q0 + Q_BLOCK
        s = jnp.einsum('bqhcd,bkhcd->bhcqk', q[:, q0:end], k[:, :end],
                       preferred_element_type=jnp.float32) * scale
        qpos = q0 + jnp.arange(Q_BLOCK)
        kpos = jnp.arange(end)
        causal = kpos[None, :] <= qpos[:, None]
        p = jax.nn.softmax(jnp.where(causal, s, -jnp.inf), axis=-1)
        w = p[:, :, 0] - lam * p[:, :, 1]
        outs.append(jnp.einsum('bhqk,bkhe->bqhe', w, v[:, :end].astype(jnp.float32)))
    return jnp.concatenate(outs, axis=1)


def causal_depthwise_conv(x, w, b):
    y = lax.conv_general_dilated(
        x, w[:, None, :].astype(x.dtype), window_strides=(1,), padding=[(CONV_WIDTH - 1, 0)],
        dimension_numbers=('NWC', 'WIO', 'NWC'), feature_group_count=x.shape[-1])
    return y + b.astype(x.dtype)


def _lin_combine(c1, c2):
    a1, b1 = c1
    a2, b2 = c2
    return a1 * a2, a2 * b1 + b2


def rg_lru(xc, w_r, b_r, w_i, b_i, lam_param):
    B, S, _ = xc.shape
    xb = xc.reshape(B, S, LRU_BLOCKS, LRU_BLOCK).astype(jnp.float32)
    r = jax.nn.sigmoid(jnp.einsum('bsnc,ncd->bsnd', xb, w_r.astype(jnp.float32)).reshape(B, S, LRU_WIDTH)
                       + b_r.astype(jnp.float32))
    gi = jax.nn.sigmoid(jnp.einsum('bsnc,ncd->bsnd', xb, w_i.astype(jnp.float32)).reshape(B, S, LRU_WIDTH)
                        + b_i.astype(jnp.float32))
    log_a = -LRU_C * r * jax.nn.softplus(-lam_param.astype(jnp.float32))
    a = jnp.exp(log_a)
    mult = jnp.sqrt(jnp.maximum(-jnp.expm1(2.0 * log_a), 0.0))
    bterm = mult * gi * xc.astype(jnp.float32)
    _, h = lax.associative_scan(_lin_combine, (a, bterm), axis=1)
    return h


def setup_inputs(seed: int = 0) -> dict:
    key = jax.random.key(seed)
    ks = jax.random.split(key, 20)
    f32 = jnp.float32
    x = jax.random.normal(ks[0], (BATCH, SEQ, D_MODEL), f32)
    pre_norm_g = 1.0 + 0.05 * jax.random.normal(ks[1], (DEPTH, D_MODEL), f32)
    w_in = jax.random.normal(ks[2], (DEPTH, D_MODEL, D_IN_PROJ), f32) * D_MODEL ** -0.5
    lambda_q1 = 0.1 * jax.random.normal(ks[3], (DEPTH, QK_HEAD_DIM), f32)
    lambda_k1 = 0.1 * jax.random.normal(ks[4], (DEPTH, QK_HEAD_DIM), f32)
    lambda_q2 = 0.1 * jax.random.normal(ks[5], (DEPTH, QK_HEAD_DIM), f32)
    lambda_k2 = 0.1 * jax.random.normal(ks[6], (DEPTH, QK_HEAD_DIM), f32)
    subln_g = 1.0 + 0.05 * jax.random.normal(ks[7], (DEPTH, V_HEAD_DIM), f32)
    conv_w = jax.random.normal(ks[8], (DEPTH, CONV_WIDTH, LRU_WIDTH), f32) * CONV_WIDTH ** -0.5
    conv_b = 0.02 * jax.random.normal(ks[9], (DEPTH, LRU_WIDTH), f32)
    w_rgate = jax.random.normal(ks[10], (DEPTH, LRU_BLOCKS, LRU_BLOCK, LRU_BLOCK), f32) * LRU_BLOCK ** -0.5
    b_rgate = 0.02 * jax.random.normal(ks[11], (DEPTH, LRU_WIDTH), f32)
    w_igate = jax.random.normal(ks[12], (DEPTH, LRU_BLOCKS, LRU_BLOCK, LRU_BLOCK), f32) * LRU_BLOCK ** -0.5
    b_igate = 0.02 * jax.random.normal(ks[13], (DEPTH, LRU_WIDTH), f32)
    a0 = jax.random.uniform(ks[14], (DEPTH, LRU_WIDTH), f32, 0.9, 0.999)
    s0 = a0 ** (1.0 / LRU_C)
    lru_lambda = jnp.log(s0) - jnp.log1p(-s0)
    w_out = jax.random.normal(ks[15], (DEPTH, D_MIX, D_MODEL), f32) * D_MIX ** -0.5
    post_norm_g = 1.0 + 0.05 * jax.random.normal(ks[16], (DEPTH, D_MODEL), f32)
    return {"x": x, "pre_norm_g": pre_norm_g, "w_in": w_in,
            "lambda_q1": lambda_q1, "lambda_k1": lambda_k1,
            "lambda_q2": lambda_q2, "lambda_k2": lambda_k2, "subln_g": subln_g,
            "conv_w": conv_w, "conv_b": conv_b,
            "w_rgate": w_rgate, "b_rgate": b_rgate, "w_igate": w_igate, "b_igate": b_igate,
            "lru_lambda": lru_lambda, "w_out": w_out, "post_norm_g": post_norm_g}


def reference(x, pre_norm_g, w_in, lambda_q1, lambda_k1, lambda_q2, lambda_k2, subln_g,
              conv_w, conv_b, w_rgate, b_rgate, w_igate, b_igate, lru_lambda, w_out, post_norm_g):
    B, S, _ = x.shape
    for l in range(DEPTH):
        h = rmsnorm(x, pre_norm_g[l])
        proj = jnp.einsum('bsd,de->bse', h, w_in[l])
        q, k, v, g_attn, x_lru, g_lru = jnp.split(proj, SPLIT_POINTS, axis=-1)

        lam_init = lambda_init_for(l)
        lam = (jnp.exp(jnp.sum(lambda_q1[l].astype(jnp.float32) * lambda_k1[l].astype(jnp.float32)))
               - jnp.exp(jnp.sum(lambda_q2[l].astype(jnp.float32) * lambda_k2[l].astype(jnp.float32)))
               + lam_init)
        qh = q.reshape(B, S, ATTN_HEADS, 2, QK_HEAD_DIM)
        kh = k.reshape(B, S, ATTN_HEADS, 2, QK_HEAD_DIM)
        vh = v.reshape(B, S, ATTN_HEADS, V_HEAD_DIM)
        o = diff_attention(qh, kh, vh, lam)
        o = rmsnorm(o, subln_g[l]) * (1.0 - lam_init)
        y_attn = o.reshape(B, S, ATTN_WIDTH).astype(x.dtype) * jax.nn.silu(g_attn)

        xc = causal_depthwise_conv(x_lru, conv_w[l], conv_b[l])
        hr = rg_lru(xc, w_rgate[l], b_rgate[l], w_igate[l], b_igate[l], lru_lambda[l])
        y_lru = hr.astype(x.dtype) * jax.nn.silu(g_lru)

        y = jnp.concatenate([y_attn, y_lru], axis=-1)
        y = jnp.einsum('bse,ed->bsd', y, w_out[l])
        x = x + rmsnorm(y, post_norm_g[l])
    return x
```

```python
import math
from contextlib import ExitStack

import numpy as np
import concourse.bass as bass
import concourse.mybir as mybir
from concourse.bass_utils import run_bass_kernel_spmd

F32 = mybir.dt.float32
BF16 = mybir.dt.bfloat16
AF = mybir.ActivationFunctionType
ALU = mybir.AluOpType

ENGS = ("pe", "act", "dve", "pool", "sp")
SYNC_SAME = ("act", "dve", "pool")
NORM_EPS = 1e-6
LRU_C = 8.0


class Tok:
    __slots__ = ("w", "rs_eng", "rs_dma", "excl")

    def __init__(self, excl=False):
        self.w = None
        self.rs_eng = {}
        self.rs_dma = []
        self.excl = excl


class Op:
    __slots__ = ("eng", "fn", "deps", "need_inc", "idx", "is_dma", "dsem", "dval", "cnt")

    def __init__(self, eng, fn, is_dma=False):
        self.eng = eng
        self.fn = fn
        self.deps = []
        self.need_inc = is_dma
        self.idx = -1
        self.is_dma = is_dma
        self.dsem = None
        self.dval = 0
        self.cnt = 0


class Sched:
    def __init__(self, n_dma_sems=8):
        self.ops = {e: [] for e in ENGS}
        self.n_dma_sems = n_dma_sems

    def _add(self, o, reads, writes):
        ex = [t for t in reads if t.excl and t not in writes]
        if ex:
            reads = [t for t in reads if not t.excl]
            writes = list(writes) + ex
        deps = {}

        def add(p):
            if p is not None and p is not o:
                deps[id(p)] = p

        for t in reads:
            add(t.w)
        for t in writes:
            add(t.w)
            for p in t.rs_eng.values():
                add(p)
            for p in t.rs_dma:
                add(p)
        best = {}
        dl = []
        for p in deps.values():
            if p.is_dma:
                dl.append(p)
            else:
                b = best.get(p.eng)
                if b is None or p.idx > b.idx:
                    best[p.eng] = p
        for e, p in best.items():
            if e == o.eng and not o.is_dma and e not in SYNC_SAME:
                continue
            dl.append(p)
        o.deps = dl
        for p in dl:
            p.need_inc = True
        for t in reads:
            if o.is_dma:
                t.rs_dma.append(o)
            else:
                t.rs_eng[o.eng] = o
        for t in writes:
            t.w = o
            t.rs_eng = {}
            t.rs_dma = []
        o.idx = len(self.ops[o.eng])
        self.ops[o.eng].append(o)
        return o

    def op(self, eng, fn, reads=(), writes=()):
        return self._add(Op(eng, fn), reads, writes)

    def dma(self, eng, fn, reads=(), writes=()):
        return self._add(Op(eng, fn, is_dma=True), reads, writes)

    def emit(self, nc, stack):
        esem = {e: stack.enter_context(nc.semaphore("s_" + e)) for e in ENGS}
        dsems = {}
        for e in ENGS:
            if any(o.is_dma for o in self.ops[e]):
                dsems[e] = [stack.enter_context(nc.semaphore("d_%s%d" % (e, i)))
                            for i in range(self.n_dma_sems)]
        for e in ENGS:
            c = 0
            tot = [0] * self.n_dma_sems
            rr = 0
            for o in self.ops[e]:
                if o.is_dma:
                    k = rr % self.n_dma_sems
                    rr += 1
                    o.dsem = (e, k)
                    o.cnt = tot[k]
                    tot[k] += 16
                    o.dval = tot[k]
                else:
                    if o.need_inc:
                        c += 1
                    o.cnt = c

        def ev(p):
            if p.is_dma:
                return dsems[p.dsem[0]][p.dsem[1]], ("d", p.dsem), p.dval
            return esem[p.eng], ("e", p.eng), p.cnt

        def run(e, eng):
            waited = {}
            for o in self.ops[e]:
                ws = []
                if o.is_dma and o.cnt > 0:
                    ws.append((dsems[e][o.dsem[1]], ("d", o.dsem), o.cnt))
                for p in o.deps:
                    ws.append(ev(p))
                for sem, key, val in ws:
                    if waited.get(key, 0) >= val:
                        continue
                    waited[key] = val
                    eng.wait_ge(sem, val)
                inst = o.fn(eng)
                if o.is_dma:
                    inst.then_inc(dsems[e][o.dsem[1]], 16)
                elif o.need_inc:
                    inst.then_inc(esem[e], 1)
            if e in dsems:
                tot = {}
                for o in self.ops[e]:
                    if o.is_dma:
                        tot[o.dsem] = max(tot.get(o.dsem, 0), o.dval)
                for (qe, k), v in tot.items():
                    if waited.get(("d", (qe, k)), 0) < v:
                        eng.wait_ge(dsems[qe][k], v)

        block = stack.enter_context(nc.Block())

        @block.tensor
        def _(eng):
            run("pe", eng)

        @block.scalar
        def _(eng):
            run("act", eng)

        @block.vector
        def _(eng):
            run("dve", eng)

        @block.gpsimd
        def _(eng):
            run("pool", eng)

        @block.sync
        def _(eng):
            run("sp", eng)


class Ring:
    def __init__(self, nc, name, n, shape, dtype, psum=False):
        alloc = nc.alloc_psum_tensor if psum else nc.alloc_sbuf_tensor
        self.tiles = [alloc("%s%d" % (name, i), shape, dtype) for i in range(n)]
        self.toks = [Tok(excl=psum) for _ in range(n)]
        self.i = 0

    def next(self):
        k = self.i % len(self.tiles)
        self.i += 1
        return self.tiles[k], self.toks[k]


def lambda_init_for(layer_idx):
    return 0.8 - 0.6 * math.exp(-0.3 * layer_idx)


def pp_layout(D):
    o = {}
    c = 0
    for name, n in (("lamv", 256), ("subg", 1), ("chan", 64), ("gprec", 8)):
        o[name] = c
        c += n
    o["_n"] = c
    return o


def build_program(L=4, NSEQ=2, S=2048, D=1024, NH=8, NC=8, dbg=False):
    nc = bass.Bass("TRN2", target_bir_lowering=False)
    DC = D // 128
    TB = S // 128
    TT = S // 512
    QT_N = S // 256
    NCC = 4 * NH + 2 * NC
    FC = NH + NC
    PPL = pp_layout(D)
    KP = PPL["_n"]

    x_d = nc.dram_tensor("x", [NSEQ, S, D], F32, kind="ExternalInput").ap()
    win_d = nc.dram_tensor("w_in", [L, NCC, 128, DC * 128], F32, kind="ExternalInput").ap()
    wout_d = nc.dram_tensor("w_out", [L, FC, 128, D], F32, kind="ExternalInput").ap()
    wg_d = nc.dram_tensor("w_g", [L, 128, 2 * NC * 128], F32, kind="ExternalInput").ap()
    pp_d = nc.dram_tensor("pp", [L, 128, KP], F32, kind="ExternalInput").ap()
    gpost_d = nc.dram_tensor("gpost", [L, 128, D], F32, kind="ExternalInput").ap()
    out_d = nc.dram_tensor("out", [NSEQ, S, D], F32, kind="ExternalOutput").ap()
    xs_d = nc.dram_tensor("xs", [NSEQ, S, D], F32, kind="Internal").ap()
    if dbg:
        dbg_hT = nc.dram_tensor("dbg_hT", [128, DC * S], F32, kind="ExternalOutput").ap()
        dbg_yT = nc.dram_tensor("dbg_yT", [128, FC * S], F32, kind="ExternalOutput").ap()

    Sc = Sched()
    A = nc.alloc_sbuf_tensor

    hT = A("hT", [128, DC, S], BF16)
    hT_tok = [Tok() for _ in range(TB)]
    yT = A("yT", [128, FC, S], BF16)
    yT_tok = [Tok() for _ in range(FC)]
    BIGN = max(FC * D, 7 * S + 8)
    big = A("big", [128, BIGN], BF16)
    wout = big[:, 0:FC * D].rearrange("p (f d) -> p f d", f=FC)
    wout_tok = [Tok() for _ in range(FC)]
    gpost = A("gpost_t", [128, D], F32)
    tGPOST = Tok()
    wg = A("wg", [128, 2 * NC * 128], BF16)
    wg_tok = Tok()
    ppt = [A("pp%d" % i, [128, KP], F32) for i in range(2)]
    pp_tok = [Tok() for _ in range(2)]
    der = [A("der%d" % i, [128, 16 + 4 * NC], F32) for i in range(2)]
    der_tok = [Tok() for _ in range(2)]
    slab_t = [A("slab%d" % i, [128, DC, 128], BF16) for i in range(8)]
    slab_k = [Tok() for _ in range(8)]
    QTs = [big[:, 0:S], A("QTb", [128, S], BF16)]
    KT0 = big[:, S:2 * S]
    KT1 = big[:, 2 * S:3 * S]
    Vt = big[:, 3 * S:4 * S].rearrange("p (a c) -> p a c", a=TB)
    sgT = big[:, 4 * S:5 * S]
    sgl = big[:, 5 * S:6 * S]
    xl = big[:, 6 * S:7 * S + 4]
    tQs = [Tok(), Tok()]
    tQ = tQs[0]
    tK, tV, tSG = Tok(), Tok(), Tok()
    pTr = Ring(nc, "pT", 4, [128, 512], BF16)
    ident_f = A("ident_f", [128, 128], F32)
    ident_b = A("ident_b", [128, 128], BF16)
    maskb = A("maskb", [128, 128], BF16)
    ones_b = A("ones_b", [128, 128], BF16)
    ones_f = A("ones_f", [1, 128], F32)
    tC = Tok()
    NW = 16
    WR = A("wr", [128, NW * 512], F32)
    wtok = [Tok() for _ in range(NW)]

    class VRing:
        def __init__(self, idx):
            self.idx = idx
            self.i = 0

        def next(self):
            k = self.idx[self.i % len(self.idx)]
            self.i += 1
            return WR[:, k * 512:(k + 1) * 512], wtok[k]
    lru_r = VRing(list(range(0, 8)))
    w512 = VRing(list(range(8, 12)))
    sil_r = VRing(list(range(12, 16)))
    assert D <= 1024
    BW = 3
    xt_v = [(WR[:, (5 * i) * 512:(5 * i) * 512 + D], wtok[5 * i:5 * i + 2]) for i in range(BW)]
    tm_v = [(WR[:, (5 * i + 2) * 512:(5 * i + 2) * 512 + D], wtok[5 * i + 2:5 * i + 4]) for i in range(BW)]
    hb_v = [(WR[:, (5 * i + 4) * 512:(5 * i + 5) * 512].bitcast(BF16)[:, 0:D], wtok[5 * i + 4:5 * i + 5]) for i in range(BW)]
    junk_ap = WR[:, 15 * 512:16 * 512].bitcast(BF16)[:, 0:D]
    jk_v = [(junk_ap, [wtok[15]]) for i in range(BW)]
    w256 = Ring(nc, "w256", 2, [128, 256], F32)
    o_r = Ring(nc, "o256", 3, [128, 256], F32)
    sqr = Ring(nc, "sq", 3, [128, 256], BF16)
    pendB = []
    ucount = [0]
    BDEF = 14

    def flushB(force=False):
        while pendB and (force or pendB[0][0] <= ucount[0]):
            pendB.pop(0)[1]()
    colr = Ring(nc, "col", 8, [128, 4], F32)
    xcbr = Ring(nc, "xcb", 2, [128, 512], BF16)
    hcar = A("hcar", [128, 4], F32)
    scr = A("scr", [128, 4], F32)
    dgw = A("dgw", [128, 4, 128], BF16)
    tXL, tSGL, tHC, tDG = [Tok() for _ in range(4)]
    psr = Ring(nc, "ps", 4, [128, 512], F32, psum=True)
    pso = Ring(nc, "pso", 2, [128, 512], F32, psum=True)
    pss = Ring(nc, "pss", 2, [128, 512], F32, psum=True)

    class BRing:
        def __init__(self, rings):
            self.items = [(t, k) for r in rings for t, k in zip(r.tiles, r.toks)]
            self.i = 0

        def next(self):
            it = self.items[self.i % len(self.items)]
            self.i += 1
            return it
    bring = BRing([psr, pso, pss])

    op = Sc.op

    op("pool", lambda e: e.memset(ident_f[:], 0.0), writes=[tC])
    op("pool", lambda e: e.affine_select(out=ident_f[:], in_=ident_f[:], compare_op=ALU.not_equal, fill=1.0,
                                         base=0, pattern=[[-1, 128]], channel_multiplier=1), reads=[tC], writes=[tC])
    op("pool", lambda e: e.tensor_copy(out=ident_b[:], in_=ident_f[:]), reads=[tC], writes=[tC])
    op("pool", lambda e: e.memset(ones_b[:], 0.0), reads=[tC], writes=[tC])
    op("pool", lambda e: e.affine_select(out=maskb[:], in_=ones_b[:], compare_op=ALU.is_ge, fill=-30000.0,
                                         base=0, pattern=[[1, 128]], channel_multiplier=-1), reads=[tC], writes=[tC])
    op("pool", lambda e: e.memset(ones_b[:], 1.0), reads=[tC], writes=[tC])
    op("pool", lambda e: e.memset(ones_f[:], 1.0), reads=[tC], writes=[tC])

    def layer_start_region():
        op("pool", lambda e: e.memset(KT0[64:128, :], 0.0), writes=[tQ, tK, tV, tSG, tXL, tSGL] + wout_tok)
        op("pool", lambda e: e.memset(KT1[0:64, :], 0.0), writes=[tK])
        op("pool", lambda e: e.memset(xl[:, 0:4], 0.0), writes=[tXL])

    def load_wg(l):
        hw_ = NC * 128
        Sc.dma("pool", lambda e: e.dma_start(out=wg[:, 0:hw_], in_=wg_d[l, :, 0:hw_]), writes=[wg_tok])
        Sc.dma("pool", lambda e: e.dma_start(out=wg[:, hw_:2 * hw_], in_=wg_d[l, :, hw_:2 * hw_]), writes=[wg_tok])

    def load_layer_params(gi, l):
        par = gi % 2
        P, tP = ppt[par], pp_tok[par]
        Dr, tD = der[par], der_tok[par]
        Sc.dma("sp", lambda e: e.dma_start(out=P[:], in_=pp_d[l]), writes=[tP])
        lv = PPL["lamv"]
        ch = PPL["chan"]
        tmp, ttmp = w256.next()
        op("dve", lambda e: e.tensor_tensor(out=tmp[:, 0:64], in0=P[:, lv:lv + 64], in1=P[:, lv + 64:lv + 128], op=ALU.mult),
           reads=[tP], writes=[ttmp])
        op("dve", lambda e: e.tensor_tensor(out=tmp[:, 64:128], in0=P[:, lv + 128:lv + 192], in1=P[:, lv + 192:lv + 256], op=ALU.mult),
           reads=[tP, ttmp], writes=[ttmp])
        op("dve", lambda e: e.reduce_sum(out=Dr[:, 0:1], in_=tmp[:, 0:64], axis=mybir.AxisListType.X), reads=[ttmp], writes=[tD])
        op("dve", lambda e: e.reduce_sum(out=Dr[:, 1:2], in_=tmp[:, 64:128], axis=mybir.AxisListType.X), reads=[ttmp, tD], writes=[tD])
        op("act", lambda e: e.activation(out=Dr[:, 2:4], in_=Dr[:, 0:2], func=AF.Exp), reads=[tD], writes=[tD])
        li = lambda_init_for(l)
        op("dve", lambda e: e.scalar_tensor_tensor(out=Dr[:, 4:5], in0=Dr[:, 3:4], scalar=-li, in1=Dr[:, 2:3],
                                                   op0=ALU.add, op1=ALU.subtract), reads=[tD], writes=[tD])
        sgc = PPL["subg"]
        op("dve", lambda e: e.tensor_scalar(out=Dr[:, 5:6], in0=P[:, sgc:sgc + 1], scalar1=(1.0 - li), scalar2=None, op0=ALU.mult),
           reads=[tP, tD], writes=[tD])
        c0 = 8
        op("dve", lambda e: e.tensor_scalar(out=Dr[:, c0:c0 + 2 * NC], in0=P[:, ch + 5 * NC:ch + 7 * NC], scalar1=-1.0, scalar2=None,
                                            op0=ALU.mult), reads=[tP, tD], writes=[tD])
        op("act", lambda e: e.activation(out=Dr[:, c0 + 3 * NC:c0 + 4 * NC], in_=P[:, ch + 7 * NC:ch + 8 * NC], func=AF.Exp, scale=-1.0),
           reads=[tP, tD], writes=[tD])
        op("act", lambda e: e.activation(out=Dr[:, c0 + 3 * NC:c0 + 4 * NC], in_=Dr[:, c0 + 3 * NC:c0 + 4 * NC], func=AF.Ln, bias=1.0),
           reads=[tD], writes=[tD])
        op("dve", lambda e: e.tensor_scalar(out=Dr[:, c0 + 2 * NC:c0 + 3 * NC], in0=Dr[:, c0 + 3 * NC:c0 + 4 * NC], scalar1=-LRU_C,
                                            scalar2=None, op0=ALU.mult), reads=[tD], writes=[tD])

    tREG = Tok()

    def load_wout(l):
        op("pool", lambda e: e.memset(scr[:], 0.0), writes=[tREG, tQ, tK, tV, tSG, tXL, tSGL])
        for fc in range(FC):
            Sc.dma("pool", (lambda fc: lambda e: e.dma_start(out=wout[:, fc, :], in_=wout_d[l, fc]))(fc),
                   reads=[tREG], writes=[wout_tok[fc]])

    def load_gpost(l):
        Sc.dma("sp", lambda e: e.dma_start(out=gpost[:], in_=gpost_d[l]), writes=[tGPOST])

    def load_slab(l, cc, slot):
        t, tk = slab_t[slot], slab_k[slot]
        Sc.dma("pool", lambda e: e.dma_start(out=t[:].rearrange("p a b -> p (a b)"), in_=win_d[l, cc]), writes=[tk])
        return t, tk

    def prenorm_block(gi, xt, txt, b, par):
        col, tcol = colr.next()
        junk, tj = jk_v[par]
        op("act", lambda e: e.activation(out=junk, in_=xt, func=AF.Square, accum_out=col[:, 0:1]),
           reads=txt, writes=tj + [tcol])
        yield
        op("act", lambda e: e.activation(out=col[:, 1:2], in_=col[:, 0:1], func=AF.Ln, scale=1.0 / D, bias=NORM_EPS),
           reads=[tcol], writes=[tcol])
        yield
        op("act", lambda e: e.activation(out=col[:, 2:3], in_=col[:, 1:2], func=AF.Exp, scale=-0.5), reads=[tcol], writes=[tcol])
        yield
        hb, thb = hb_v[par]
        P, tP = ppt[gi % 2], pp_tok[gi % 2]
        gc = PPL["gprec"]
        op("dve", lambda e: e.tensor_scalar(out=hb, in0=xt, scalar1=col[:, 2:3], scalar2=None, op0=ALU.mult),
           reads=txt + [tcol], writes=thb)
        yield
        ps, tps = bring.next()
        psb = ps[:].bitcast(BF16)

        def tr(e):
            inst = None
            for i in range(DC):
                inst = e.transpose(psb[:, i * 128:(i + 1) * 128], hb[:, i * 128:(i + 1) * 128], ident_b[:])
            return inst
        op("pe", tr, reads=thb + [tC], writes=[tps])
        def ev(e):
            inst = None
            for i in range(DC):
                inst = e.tensor_scalar(out=hT[:, i, b * 128:(b + 1) * 128], in0=psb[:, i * 128:(i + 1) * 128],
                                       scalar1=P[:, gc + i:gc + i + 1], scalar2=None, op0=ALU.mult)
            return inst
        op("dve", ev, reads=[tps, tP], writes=[hT_tok[b]])
        yield

    def proj_fm(sl, tsl, evac):
        for tt in range(TT):
            ps, tps = psr.next()

            def mm(e, tt=tt, ps=ps):
                inst = None
                for dc in range(DC):
                    inst = e.matmul(ps[:, :], lhsT=sl[:, dc, :], rhs=hT[:, dc, tt * 512:(tt + 1) * 512],
                                    start=(dc == 0), stop=(dc == DC - 1))
                return inst
            op("pe", mm, reads=[tsl] + hT_tok[tt * 4:(tt + 1) * 4], writes=[tps])
            evac(tt, ps, tps)
            yield

    def silu_evac(dest, tdest):
        def evac(tt, ps, tps):
            E, tE = sil_r.next()
            G, tG = sil_r.next()
            op("act", lambda e: e.activation(out=E[:], in_=ps[:], func=AF.Exp, scale=-1.0), reads=[tps], writes=[tE])
            op("dve", lambda e: e.tensor_copy(out=G[:], in_=ps[:]), reads=[tps], writes=[tG])
            op("act", lambda e: e.activation(out=E[:], in_=E[:], func=AF.Ln, bias=1.0), reads=[tE], writes=[tE])
            op("act", lambda e: e.activation(out=E[:], in_=E[:], func=AF.Exp, scale=-1.0), reads=[tE], writes=[tE])
            op("pool", lambda e: e.tensor_tensor(out=dest[:, tt * 512:(tt + 1) * 512], in0=G[:], in1=E[:], op=ALU.mult),
               reads=[tG, tE], writes=[tdest])
        return evac

    def run_all(g):
        for _ in g:
            pass

    def q_proj(slq, tslq, par):
        def evq(tt, ps, tps):
            op("dve", lambda e: e.tensor_copy(out=QTs[par][:, tt * 512:(tt + 1) * 512], in_=ps[:]), reads=[tps], writes=[tQs[par]])
        return proj_fm(slq, tslq, evq)

    def head_stage(gi, l, h, slabs, side=None, prefetch=None, qnext=None):
        P, tP = ppt[gi % 2], pp_tok[gi % 2]
        Dr, tD = der[gi % 2], der_tok[gi % 2]
        (slq, tslq), (slk, tslk), (slv, tslv), (slg, tslg) = slabs
        sg0 = PPL["subg"]
        li = lambda_init_for(l)

        QTt, tQh = QTs[h % 2], tQs[h % 2]
        if h == 0:
            run_all(q_proj(slq, tslq, 0))

        def evk(tt, ps, tps):
            op("act", lambda e: e.activation(out=KT0[0:64, tt * 512:(tt + 1) * 512], in_=ps[0:64, :], func=AF.Copy),
               reads=[tps], writes=[tK])
            op("act", lambda e: e.activation(out=KT1[64:128, tt * 512:(tt + 1) * 512], in_=ps[64:128, :], func=AF.Copy),
               reads=[tps], writes=[tK])
        run_all(proj_fm(slk, tslk, evk))
        flushB(force=True)
        run_all(proj_fm(slg, tslg, silu_evac(sgT, tSG)))
        for k4 in range(TB // 4):
            ps, tps = psr.next()

            def mmv(e, k4=k4, ps=ps):
                inst = None
                for i in range(4):
                    kb = k4 * 4 + i
                    for dc in range(DC):
                        inst = e.matmul(ps[:, i * 128:(i + 1) * 128], lhsT=hT[:, dc, kb * 128:(kb + 1) * 128], rhs=slv[:, dc, :],
                                        start=(dc == 0), stop=(dc == DC - 1))
                return inst
            op("pe", mmv, reads=[tslv] + hT_tok[k4 * 4:(k4 + 1) * 4], writes=[tps])
            op("dve", lambda e, k4=k4, ps=ps: e.tensor_copy(out=Vt[:, k4 * 4:(k4 + 1) * 4, :],
                                                            in_=ps[:, :].rearrange("p (a c) -> p a c", a=4)),
               reads=[tps], writes=[tV])

        if prefetch is not None:
            prefetch()
        units = [(j, kb) for j in range(QT_N) for kb in range(2 * j + 2)]
        LOOK = 2
        st = {}

        def emit_qk(u):
            j, kb = units[u]
            q0 = 128 if kb == 2 * j + 1 else 0
            diag = (kb - 2 * j) if kb >= 2 * j else None
            ps, tps = psr.next()

            def qk(e):
                inst = None
                for c in (0, 1):
                    KTc = KT0 if c == 0 else KT1
                    inst = e.matmul(ps[:, c * 256 + q0:(c + 1) * 256], lhsT=KTc[:, kb * 128:(kb + 1) * 128],
                                    rhs=QTt[:, j * 256 + q0:(j + 1) * 256], start=True, stop=(diag is None))
                    if diag is not None:
                        inst = e.matmul(ps[:, c * 256 + diag * 128:c * 256 + (diag + 1) * 128], lhsT=ident_b[:], rhs=maskb[:],
                                        start=False, stop=True)
                return inst
            op("pe", qk, reads=[tK, tQh, tC], writes=[tps])
            pt, tpt = pTr.next()
            if q0 == 0:
                op("act", lambda e: e.activation(out=pt[:, :], in_=ps[:, :], func=AF.Exp, scale=0.125), reads=[tps], writes=[tpt])
            else:
                op("act", lambda e: e.activation(out=pt[:, :].rearrange("p (c q) -> p c q", c=2)[:, :, 128:256],
                                                 in_=ps[:, :].rearrange("p (c q) -> p c q", c=2)[:, :, 128:256],
                                                 func=AF.Exp, scale=0.125), reads=[tps], writes=[tpt])
            st[u] = (pt, tpt, q0)

        def emit_pv(u):
            j, kb = units[u]
            pt, tpt, q0 = st.pop(u)
            if kb == 0:
                st["po"] = pso.next()
                st["pS"] = pss.next()
            po, tpo = st["po"]
            pS, tpS = st["pS"]
            first, last = (kb == 0), (kb == 2 * j + 1)

            def pv(e):
                if q0 == 0:
                    e.matmul(po[:, :], lhsT=Vt[:, kb, :], rhs=pt[:, :], start=first, stop=last)
                    return e.matmul(pS[:, :], lhsT=ones_b[:], rhs=pt[:, :], start=first, stop=last)
                inst = None
                for c in (0, 1):
                    sl_ = slice(c * 256 + 128, c * 256 + 256)
                    e.matmul(po[:, sl_], lhsT=Vt[:, kb, :], rhs=pt[:, sl_], start=first, stop=(last and c == 1))
                    inst = e.matmul(pS[:, sl_], lhsT=ones_b[:], rhs=pt[:, sl_], start=first, stop=(last and c == 1))
                return inst
            op("pe", pv, reads=[tV, tpt, tC], writes=[tpo, tpS])
            if not last:
                return
            R, tR = w512.next()
            op("dve", lambda e: e.reciprocal(out=R[:], in_=pS[:, :]), reads=[tpS], writes=[tR])
            T, tT = w512.next()
            op("dve", lambda e: e.tensor_tensor(out=T[:], in0=po[:, :], in1=R[:], op=ALU.mult), reads=[tpo, tR], writes=[tT])
            o, to = o_r.next()
            op("dve", lambda e: e.scalar_tensor_tensor(out=o[:], in0=T[:, 256:512], scalar=Dr[:, 4:5], in1=T[:, 0:256],
                                                       op0=ALU.mult, op1=ALU.add), reads=[tT, tD], writes=[to])
            sq, tsq = sqr.next()
            op("pool", lambda e: e.tensor_tensor(out=sq[:], in0=o[:], in1=o[:], op=ALU.mult), reads=[to], writes=[tsq])

            def partB():
                psm, tpsm = psr.next()
                op("pe", lambda e: e.matmul(psm[:, 0:256], lhsT=ones_b[:], rhs=sq[:], start=True, stop=True),
                   reads=[tsq, tC], writes=[tpsm])
                rs, trs = w256.next()
                op("act", lambda e: e.activation(out=rs[:], in_=psm[:, 0:256], func=AF.Ln, scale=1.0 / 128, bias=NORM_EPS),
                   reads=[tpsm], writes=[trs])
                op("act", lambda e: e.activation(out=rs[:], in_=rs[:], func=AF.Exp, scale=-0.5), reads=[trs], writes=[trs])
                y, ty = w256.next()
                op("dve", lambda e: e.scalar_tensor_tensor(out=y[:], in0=o[:], scalar=Dr[:, 5:6], in1=rs[:],
                                                           op0=ALU.mult, op1=ALU.mult), reads=[to, tD, trs], writes=[ty])
                op("pool", lambda e: e.tensor_tensor(out=yT[:, h, j * 256:(j + 1) * 256], in0=y[:], in1=sgT[:, j * 256:(j + 1) * 256],
                                                     op=ALU.mult), reads=[ty, tSG], writes=[yT_tok[h]])
            assert len(pendB) < 3
            pendB.append((ucount[0] + BDEF, partB))

        nU = len(units)
        side3 = qnext() if qnext is not None else None
        q_every = max(1, nU // (TT + 1))
        for u in range(min(LOOK, nU)):
            emit_qk(u)
        for u in range(nU):
            if u + LOOK < nU:
                emit_qk(u + LOOK)
            emit_pv(u)
            ucount[0] += 1
            flushB()
            if side is not None:
                next(side, None)
            if side3 is not None and (u + 1) % q_every == 0:
                next(side3, None)
        if side is not None:
            run_all(side)
        if side3 is not None:
            run_all(side3)

    def lru_stage(gi, l, n, slabs):
        P, tP = ppt[gi % 2], pp_tok[gi % 2]
        Dr, tD = der[gi % 2], der_tok[gi % 2]
        (slx, tslx), (slg2, tslg2) = slabs
        ch = PPL["chan"]
        c0 = 8
        for jj in range(4):
            op("dve", lambda e, jj=jj: e.tensor_scalar(out=dgw[:, jj, :], in0=ident_f[:], scalar1=P[:, ch + jj * NC + n:ch + jj * NC + n + 1],
                                                       scalar2=None, op0=ALU.mult), reads=[tP, tC], writes=[tDG])

        def evx(tt, ps, tps):
            op("dve", lambda e: e.tensor_copy(out=xl[:, 4 + tt * 512:4 + (tt + 1) * 512], in_=ps[:]), reads=[tps], writes=[tXL])
        yield from proj_fm(slx, tslx, evx)
        yield from proj_fm(slg2, tslg2, silu_evac(sgl, tSGL))
        def lru_tile(tt):
            ts = slice(tt * 512, (tt + 1) * 512)
            ps, tps = psr.next()

            def cv(e, tt=tt, ps=ps):
                inst = None
                for jj in range(4):
                    inst = e.matmul(ps[:, :], lhsT=dgw[:, jj, :], rhs=xl[:, 1 + jj + tt * 512:1 + jj + (tt + 1) * 512],
                                    start=(jj == 0), stop=(jj == 3))
                return inst
            op("pe", cv, reads=[tDG, tXL], writes=[tps])
            xcf, tXCF = lru_r.next()
            xcb, tXCB = xcbr.next()
            op("act", lambda e, ps=ps, xcf=xcf: e.activation(out=xcf[:], in_=ps[:, :], func=AF.Identity,
                                                             bias=P[:, ch + 4 * NC + n:ch + 4 * NC + n + 1]),
               reads=[tps, tP], writes=[tXCF])
            yield
            op("dve", lambda e, xcf=xcf, xcb=xcb: e.tensor_copy(out=xcb[:], in_=xcf[:]), reads=[tXCF], writes=[tXCB])
            yield
            yield
            pr, tpr = psr.next()
            op("pe", lambda e, pr=pr, xcb=xcb: e.matmul(pr[:, :], lhsT=wg[:, n * 128:(n + 1) * 128], rhs=xcb[:], start=True, stop=True),
               reads=[wg_tok, tXCB], writes=[tpr])
            pi, tpi = psr.next()
            op("pe", lambda e, pi=pi, xcb=xcb: e.matmul(pi[:, :], lhsT=wg[:, (NC + n) * 128:(NC + n + 1) * 128], rhs=xcb[:],
                                                        start=True, stop=True), reads=[wg_tok, tXCB], writes=[tpi])
            t1, tt1 = lru_r.next()
            t2, tt2 = lru_r.next()
            t3, tt3 = lru_r.next()
            a, ta = t1, tt1
            nbr = Dr[:, c0 + n:c0 + n + 1]
            nbi = Dr[:, c0 + NC + n:c0 + NC + n + 1]
            cneg = Dr[:, c0 + 2 * NC + n:c0 + 2 * NC + n + 1]
            op("act", lambda e, pr=pr: e.activation(out=t1[:], in_=pr[:, :], func=AF.Exp, scale=-1.0, bias=nbr), reads=[tpr, tD], writes=[tt1])
            op("act", lambda e, pi=pi: e.activation(out=t3[:], in_=pi[:, :], func=AF.Exp, scale=-1.0, bias=nbi), reads=[tpi, tD], writes=[tt3])
            yield
            op("act", lambda e: e.activation(out=t1[:], in_=t1[:], func=AF.Ln, bias=1.0), reads=[tt1], writes=[tt1])
            op("act", lambda e: e.activation(out=t3[:], in_=t3[:], func=AF.Ln, bias=1.0), reads=[tt3], writes=[tt3])
            yield
            op("act", lambda e: e.activation(out=t1[:], in_=t1[:], func=AF.Exp, scale=-1.0), reads=[tt1], writes=[tt1])
            yield
            op("act", lambda e: e.activation(out=a[:], in_=t1[:], func=AF.Exp, scale=cneg), reads=[tt1, tD], writes=[ta])
            yield
            op("pool", lambda e: e.tensor_tensor(out=t2[:], in0=a[:], in1=a[:], op=ALU.mult), reads=[ta], writes=[tt2])
            yield
            op("act", lambda e: e.activation(out=t2[:], in_=t2[:], func=AF.Ln, scale=-1.0, bias=1.0000001), reads=[tt2], writes=[tt2])
            yield
            op("dve", lambda e: e.scalar_tensor_tensor(out=t2[:], in0=t2[:], scalar=0.5, in1=t3[:], op0=ALU.mult, op1=ALU.subtract),
               reads=[tt2, tt3], writes=[tt2])
            yield
            op("act", lambda e: e.activation(out=t2[:], in_=t2[:], func=AF.Exp), reads=[tt2], writes=[tt2])
            yield
            op("pool", lambda e, xcf=xcf: e.tensor_tensor(out=t2[:], in0=t2[:], in1=xcf[:], op=ALU.mult), reads=[tt2, tXCF], writes=[tt2])
            yield
            hs, tHS = t3, tt3
            init = 0.0 if tt == 0 else hcar[:, (tt - 1) % 4:(tt - 1) % 4 + 1]
            op("dve", lambda e, hs=hs, init=init: e.tensor_tensor_scan(out=hs[:], data0=a[:], data1=t2[:], initial=init,
                                                                       op0=ALU.mult, op1=ALU.add), reads=[ta, tt2, tHC], writes=[tHS])
            op("dve", lambda e, hs=hs, tt=tt: e.tensor_copy(out=hcar[:, tt % 4:tt % 4 + 1], in_=hs[:, 511:512]), reads=[tHS], writes=[tHC])
            op("pool", lambda e, ts=ts, hs=hs: e.tensor_tensor(out=yT[:, NH + n, ts], in0=hs[:], in1=sgl[:, ts], op=ALU.mult),
               reads=[tHS, tSGL], writes=[yT_tok[NH + n]])

        for t0 in range(0, TT, 2):
            gens = [lru_tile(tt) for tt in range(t0, min(TT, t0 + 2))]
            while gens:
                for g in list(gens):
                    try:
                        next(g)
                    except StopIteration:
                        gens.remove(g)
                        continue
                    yield

    DW = min(512, D)
    NHALF = D // DW
    xs_tok = [[Tok() for _ in range(TB)] for _ in range(NSEQ)]

    def boundary_block(gi, s, l, b):
        par = b % BW
        pss_ = [bring.next() for _ in range(NHALF)]

        def mmo(e):
            inst = None
            for hf in range(NHALF):
                for fc in range(FC):
                    inst = e.matmul(pss_[hf][0][:, 0:DW], lhsT=yT[:, fc, b * 128:(b + 1) * 128], rhs=wout[:, fc, hf * DW:(hf + 1) * DW],
                                    start=(fc == 0), stop=(fc == FC - 1))
            return inst
        op("pe", mmo, reads=yT_tok + wout_tok, writes=[t for _, t in pss_])
        xt, txt = xt_v[par]
        src = x_d if l == 0 else xs_d
        dst = out_d if l == L - 1 else xs_d
        rd = [xs_tok[s][b]] if l > 0 else []
        Sc.dma("sp", lambda e: e.dma_start(out=xt, in_=src[s, b * 128:(b + 1) * 128, :]), reads=rd, writes=txt)
        yield
        col, tcol = colr.next()
        junk, tj = jk_v[par]
        for hf in range(NHALF):
            op("act", lambda e, hf=hf: e.activation(out=junk[:, hf * DW:(hf + 1) * DW], in_=pss_[hf][0][:, 0:DW], func=AF.Square,
                                                    accum_out=col[:, hf:hf + 1]), reads=[pss_[hf][1]], writes=tj + [tcol])
        yield
        if NHALF == 2:
            op("dve", lambda e: e.tensor_tensor(out=col[:, 0:1], in0=col[:, 0:1], in1=col[:, 1:2], op=ALU.add), reads=[tcol], writes=[tcol])
            yield
        op("act", lambda e: e.activation(out=col[:, 2:3], in_=col[:, 0:1], func=AF.Ln, scale=1.0 / D, bias=NORM_EPS),
           reads=[tcol], writes=[tcol])
        yield
        op("act", lambda e: e.activation(out=col[:, 3:4], in_=col[:, 2:3], func=AF.Exp, scale=-0.5), reads=[tcol], writes=[tcol])
        yield
        tm, ttm = tm_v[par]
        for hf in range(NHALF):
            op("dve", lambda e, hf=hf: e.scalar_tensor_tensor(out=tm[:, hf * DW:(hf + 1) * DW], in0=pss_[hf][0][:, 0:DW], scalar=col[:, 3:4],
                                                              in1=gpost[:, hf * DW:(hf + 1) * DW], op0=ALU.mult, op1=ALU.mult),
               reads=[pss_[hf][1], tcol, tGPOST], writes=ttm)
        yield
        op("dve", lambda e: e.tensor_tensor(out=xt, in0=xt, in1=tm, op=ALU.add), reads=txt + ttm, writes=txt)
        yield
        Sc.dma("sp", lambda e: e.dma_start(out=dst[s, b * 128:(b + 1) * 128, :], in_=xt), reads=txt,
               writes=([xs_tok[s][b]] if l < L - 1 else []))
        if l < L - 1:
            yield from prenorm_block(gi + 1, xt, txt, b, par)

    def interleave(gens, width=2, stagger=0):
        gens = list(gens)
        active = []
        while gens or active:
            if gens and len(active) < width and (not active or active[-1][1] >= stagger):
                active.append([gens.pop(0), 0])
            for it in list(active):
                try:
                    next(it[0])
                    it[1] += 1
                except StopIteration:
                    active.remove(it)

    assert NH == NC
    pairs = [(s_, l_, i_) for s_ in range(NSEQ) for l_ in range(L) for i_ in range(NH)]
    pair_slabs = {}

    def load_pair(k):
        if k >= len(pairs):
            return
        _, l_, i_ = pairs[k]
        lo = 4 + 2 * (k % 2)
        pair_slabs[k] = ([load_slab(l_, i_, 0), load_slab(l_, NH + i_, 1), load_slab(l_, 2 * NH + i_, 2), load_slab(l_, 3 * NH + i_, 3)],
                         [load_slab(l_, 4 * NH + i_, lo), load_slab(l_, 4 * NH + NC + i_, lo + 1)])

    load_pair(0)
    k = 0
    for s in range(NSEQ):
        gi0 = s * L
        load_layer_params(gi0, 0)

        def first_pre(b, s=s):
            xt, txt = xt_v[b % BW]
            Sc.dma("sp", lambda e: e.dma_start(out=xt, in_=x_d[s, b * 128:(b + 1) * 128, :]), writes=txt)
            yield
            yield from prenorm_block(gi0, xt, txt, b, b % BW)
        interleave([first_pre(b) for b in range(TB)], width=BW, stagger=2)
        for l in range(L):
            gi = gi0 + l
            layer_start_region()
            load_wg(l)
            load_gpost(l)
            if l + 1 < L:
                load_layer_params(gi + 1, l + 1)
            for i in range(NH):
                hsl, lsl = pair_slabs.pop(k)
                side = lru_stage(gi, l, i, lsl)
                qn = None
                if i + 1 < NH:
                    qn = (lambda k=k, i=i: q_proj(pair_slabs[k + 1][0][0][0], pair_slabs[k + 1][0][0][1], (i + 1) % 2))
                head_stage(gi, l, i, hsl, side=side, prefetch=(lambda k=k: load_pair(k + 1)), qnext=qn)
                k += 1
            flushB(force=True)
            load_wout(l)
            interleave([boundary_block(gi, s, l, b) for b in range(TB)], width=BW, stagger=4)

    if dbg:
        pass
    with ExitStack() as stack:
        Sc.emit(nc, stack)
    return nc


def pack_weights(inp, L, D, NH, NC):
    DC = D // 128
    W = NH * 128
    WL = NC * 128
    w_in = np.asarray(inp["w_in"], np.float32)
    ncc = 4 * NH + 2 * NC
    w_in_r = np.ascontiguousarray(w_in.reshape(L, DC, 128, ncc, 128).transpose(0, 3, 2, 1, 4)).reshape(L, ncc, 128, DC * 128)
    w_out = np.asarray(inp["w_out"], np.float32)
    w_out_r = np.ascontiguousarray(w_out.reshape(L, NH + NC, 128, D))
    wr = np.asarray(inp["w_rgate"], np.float32).transpose(0, 2, 1, 3).reshape(L, 128, NC * 128)
    wi = np.asarray(inp["w_igate"], np.float32).transpose(0, 2, 1, 3).reshape(L, 128, NC * 128)
    w_g = np.ascontiguousarray(np.concatenate([wr, wi], axis=2))
    PPL = pp_layout(D)
    pp = np.zeros((L, 128, PPL["_n"]), np.float32)
    pp_gpre = np.asarray(inp["pre_norm_g"], np.float32).reshape(L, DC, 128).transpose(0, 2, 1)
    gpost = np.ascontiguousarray(np.broadcast_to(np.asarray(inp["post_norm_g"], np.float32)[:, None, :], (L, 128, D)))
    lv = PPL["lamv"]
    for i, k in enumerate(("lambda_q1", "lambda_k1", "lambda_q2", "lambda_k2")):
        pp[:, :, lv + 64 * i:lv + 64 * (i + 1)] = np.asarray(inp[k], np.float32)[:, None, :]
    pp[:, :, PPL["subg"]] = np.asarray(inp["subln_g"], np.float32)
    ch = PPL["chan"]
    cw = np.asarray(inp["conv_w"], np.float32)
    chan = [cw[:, 0], cw[:, 1], cw[:, 2], cw[:, 3], inp["conv_b"], inp["b_rgate"], inp["b_igate"], inp["lru_lambda"]]
    for k, v in enumerate(chan):
        v = np.asarray(v, np.float32).reshape(L, NC, 128).transpose(0, 2, 1)
        pp[:, :, ch + k * NC:ch + (k + 1) * NC] = v
    pp[:, :, PPL["gprec"]:PPL["gprec"] + DC] = pp_gpre
    return {"w_in": w_in_r, "w_out": w_out_r, "w_g": w_g, "pp": pp, "gpost": gpost}


_NC_CACHE = {}


def kernel(**inputs):
    L, D, NH, NC, S = 4, 1024, 8, 8, 2048
    x = np.asarray(inputs["x"], np.float32)
    B = x.shape[0]
    n_cores = 8
    nseq = B // n_cores
    wts = pack_weights(inputs, L, D, NH, NC)
    key = (L, nseq, S, D, NH, NC)
    if key not in _NC_CACHE:
        _NC_CACHE[key] = build_program(L=L, NSEQ=nseq, S=S, D=D, NH=NH, NC=NC)
    nc = _NC_CACHE[key]
    in_maps = []
    for c in range(n_cores):
        m = {"x": np.ascontiguousarray(x[c * nseq:(c + 1) * nseq])}
        m.update(wts)
        in_maps.append(m)
    res = run_bass_kernel_spmd(nc, in_maps, core_ids=list(range(n_cores)))
    return np.concatenate([np.asarray(r["out"], np.float32) for r in res.results], axis=0)
```

```python
import math
from contextlib import ExitStack

import numpy as np
import concourse.bass as bass
import concourse.mybir as mybir
from concourse.bass_utils import run_bass_kernel_spmd

F32 = mybir.dt.float32
BF16 = mybir.dt.bfloat16
AF = mybir.ActivationFunctionType
ALU = mybir.AluOpType

ENGS = ("pe", "act", "dve", "pool", "sp")
SYNC_SAME = ("act", "dve")
NORM_EPS = 1e-6
LRU_C = 8.0


class Tok:
    __slots__ = ("w", "rs_eng", "rs_dma", "excl")

    def __init__(self, excl=False):
        self.w = None
        self.rs_eng = {}
        self.rs_dma = []
        self.excl = excl


class Op:
    __slots__ = ("eng", "fn", "deps", "need_inc", "idx", "is_dma", "dsem", "dval", "cnt")

    def __init__(self, eng, fn, is_dma=False):
        self.eng = eng
        self.fn = fn
        self.deps = []
        self.need_inc = is_dma
        self.idx = -1
        self.is_dma = is_dma
        self.dsem = None
        self.dval = 0
        self.cnt = 0


class Sched:
    def __init__(self, n_dma_sems=8):
        self.ops = {e: [] for e in ENGS}
        self.n_dma_sems = n_dma_sems

    def _add(self, o, reads, writes):
        ex = [t for t in reads if t.excl and t not in writes]
        if ex:
            reads = [t for t in reads if not t.excl]
            writes = list(writes) + ex
        deps = {}

        def add(p):
            if p is not None and p is not o:
                deps[id(p)] = p

        for t in reads:
            add(t.w)
        for t in writes:
            add(t.w)
            for p in t.rs_eng.values():
                add(p)
            for p in t.rs_dma:
                add(p)
        best = {}
        dl = []
        for p in deps.values():
            if p.is_dma:
                dl.append(p)
            else:
                b = best.get(p.eng)
                if b is None or p.idx > b.idx:
                    best[p.eng] = p
        for e, p in best.items():
            if e == o.eng and not o.is_dma and e not in SYNC_SAME:
                continue
            dl.append(p)
        o.deps = dl
        for p in dl:
            p.need_inc = True
        for t in reads:
            if o.is_dma:
                t.rs_dma.append(o)
            else:
                t.rs_eng[o.eng] = o
        for t in writes:
            t.w = o
            t.rs_eng = {}
            t.rs_dma = []
        o.idx = len(self.ops[o.eng])
        self.ops[o.eng].append(o)
        return o

    def op(self, eng, fn, reads=(), writes=()):
        return self._add(Op(eng, fn), reads, writes)

    def dma(self, eng, fn, reads=(), writes=()):
        return self._add(Op(eng, fn, is_dma=True), reads, writes)

    def emit(self, nc, stack):
        esem = {e: stack.enter_context(nc.semaphore("s_" + e)) for e in ENGS}
        dsems = {}
        for e in ENGS:
            if any(o.is_dma for o in self.ops[e]):
                dsems[e] = [stack.enter_context(nc.semaphore("d_%s%d" % (e, i)))
                            for i in range(self.n_dma_sems)]
        for e in ENGS:
            c = 0
            tot = [0] * self.n_dma_sems
            rr = 0
            for o in self.ops[e]:
                if o.is_dma:
                    k = rr % self.n_dma_sems
                    rr += 1
                    o.dsem = (e, k)
                    o.cnt = tot[k]
                    tot[k] += 16
                    o.dval = tot[k]
                else:
                    if o.need_inc:
                        c += 1
                    o.cnt = c

        def ev(p):
            if p.is_dma:
                return dsems[p.dsem[0]][p.dsem[1]], ("d", p.dsem), p.dval
            return esem[p.eng], ("e", p.eng), p.cnt

        def run(e, eng):
            waited = {}
            for o in self.ops[e]:
                ws = []
                if o.is_dma and o.cnt > 0:
                    ws.append((dsems[e][o.dsem[1]], ("d", o.dsem), o.cnt))
                for p in o.deps:
                    ws.append(ev(p))
                for sem, key, val in ws:
                    if waited.get(key, 0) >= val:
                        continue
                    waited[key] = val
                    eng.wait_ge(sem, val)
                inst = o.fn(eng)
                if o.is_dma:
                    inst.then_inc(dsems[e][o.dsem[1]], 16)
                elif o.need_inc:
                    inst.then_inc(esem[e], 1)
            if e in dsems:
                tot = {}
                for o in self.ops[e]:
                    if o.is_dma:
                        tot[o.dsem] = max(tot.get(o.dsem, 0), o.dval)
                for (qe, k), v in tot.items():
                    if waited.get(("d", (qe, k)), 0) < v:
                        eng.wait_ge(dsems[qe][k], v)

        block = stack.enter_context(nc.Block())

        @block.tensor
        def _(eng):
            run("pe", eng)

        @block.scalar
        def _(eng):
            run("act", eng)

        @block.vector
        def _(eng):
            run("dve", eng)

        @block.gpsimd
        def _(eng):
            run("pool", eng)

        @block.sync
        def _(eng):
            run("sp", eng)


class Ring:
    def __init__(self, nc, name, n, shape, dtype, psum=False):
        alloc = nc.alloc_psum_tensor if psum else nc.alloc_sbuf_tensor
        self.tiles = [alloc("%s%d" % (name, i), shape, dtype) for i in range(n)]
        self.toks = [Tok(excl=psum) for _ in range(n)]
        self.i = 0

    def next(self):
        k = self.i % len(self.tiles)
        self.i += 1
        return self.tiles[k], self.toks[k]


def lambda_init_for(layer_idx):
    return 0.8 - 0.6 * math.exp(-0.3 * layer_idx)


def pp_layout(D):
    o = {}
    c = 0
    for name, n in (("lamv", 256), ("subg", 1), ("chan", 64), ("gprec", 8)):
        o[name] = c
        c += n
    o["_n"] = c
    return o


def build_program(L=4, NSEQ=2, S=2048, D=1024, NH=8, NC=8, dbg=False):
    nc = bass.Bass("TRN2", target_bir_lowering=False)
    DC = D // 128
    TB = S // 128
    TT = S // 512
    QT_N = S // 256
    NCC = 4 * NH + 2 * NC
    FC = NH + NC
    PPL = pp_layout(D)
    KP = PPL["_n"]

    x_d = nc.dram_tensor("x", [NSEQ, S, D], F32, kind="ExternalInput").ap()
    win_d = nc.dram_tensor("w_in", [L, NCC, 128, DC * 128], F32, kind="ExternalInput").ap()
    wout_d = nc.dram_tensor("w_out", [L, FC, 128, D], F32, kind="ExternalInput").ap()
    wg_d = nc.dram_tensor("w_g", [L, 128, 2 * NC * 128], F32, kind="ExternalInput").ap()
    pp_d = nc.dram_tensor("pp", [L, 128, KP], F32, kind="ExternalInput").ap()
    gpost_d = nc.dram_tensor("gpost", [L, 128, D], F32, kind="ExternalInput").ap()
    out_d = nc.dram_tensor("out", [NSEQ, S, D], F32, kind="ExternalOutput").ap()
    xs_d = nc.dram_tensor("xs", [NSEQ, S, D], F32, kind="Internal").ap()
    if dbg:
        dbg_hT = nc.dram_tensor("dbg_hT", [128, DC * S], F32, kind="ExternalOutput").ap()
        dbg_yT = nc.dram_tensor("dbg_yT", [128, FC * S], F32, kind="ExternalOutput").ap()

    Sc = Sched()
    A = nc.alloc_sbuf_tensor

    hT = A("hT", [128, DC, S], BF16)
    hT_tok = [Tok() for _ in range(TB)]
    yT = A("yT", [128, FC, S], BF16)
    yT_tok = [Tok() for _ in range(FC)]
    BIGN = max(FC * D, 7 * S + 8)
    big = A("big", [128, BIGN], BF16)
    wout = big[:, 0:FC * D].rearrange("p (f d) -> p f d", f=FC)
    wout_tok = [Tok() for _ in range(FC)]
    gpost = A("gpost_t", [128, D], F32)
    tGPOST = Tok()
    wg = A("wg", [128, 2 * NC * 128], BF16)
    wg_tok = Tok()
    ppt = [A("pp%d" % i, [128, KP], F32) for i in range(2)]
    pp_tok = [Tok() for _ in range(2)]
    der = [A("der%d" % i, [128, 16 + 4 * NC], F32) for i in range(2)]
    der_tok = [Tok() for _ in range(2)]
    slab_t = [A("slab%d" % i, [128, DC, 128], BF16) for i in range(8)]
    slab_k = [Tok() for _ in range(8)]
    QTs = [big[:, 0:S], A("QTb", [128, S], BF16)]
    KT0 = big[:, S:2 * S]
    KT1 = big[:, 2 * S:3 * S]
    Vt = big[:, 3 * S:4 * S].rearrange("p (a c) -> p a c", a=TB)
    sgT = big[:, 4 * S:5 * S]
    sgl = big[:, 5 * S:6 * S]
    xl = big[:, 6 * S:7 * S + 4]
    tQs = [Tok(), Tok()]
    tQ = tQs[0]
    tK, tV, tSG = Tok(), Tok(), Tok()
    pTr = Ring(nc, "pT", 4, [128, 512], BF16)
    ident_f = A("ident_f", [128, 128], F32)
    ident_b = A("ident_b", [128, 128], BF16)
    maskb = A("maskb", [128, 128], BF16)
    ones_b = A("ones_b", [128, 128], BF16)
    ones_f = A("ones_f", [1, 128], F32)
    tC = Tok()
    NW = 16
    WR = A("wr", [128, NW * 512], F32)
    wtok = [Tok() for _ in range(NW)]

    class VRing:
        def __init__(self, idx):
            self.idx = idx
            self.i = 0

        def next(self):
            k = self.idx[self.i % len(self.idx)]
            self.i += 1
            return WR[:, k * 512:(k + 1) * 512], wtok[k]
    lru_r = VRing(list(range(0, 8)))
    w512 = VRing(list(range(8, 12)))
    sil_r = VRing(list(range(12, 16)))
    assert D <= 1024
    BW = 3
    xt_v = [(WR[:, (5 * i) * 512:(5 * i) * 512 + D], wtok[5 * i:5 * i + 2]) for i in range(BW)]
    tm_v = [(WR[:, (5 * i + 2) * 512:(5 * i + 2) * 512 + D], wtok[5 * i + 2:5 * i + 4]) for i in range(BW)]
    hb_v = [(WR[:, (5 * i + 4) * 512:(5 * i + 5) * 512].bitcast(BF16)[:, 0:D], wtok[5 * i + 4:5 * i + 5]) for i in range(BW)]
    junk_ap = WR[:, 15 * 512:16 * 512].bitcast(BF16)[:, 0:D]
    jk_v = [(junk_ap, [wtok[15]]) for i in range(BW)]
    w256 = Ring(nc, "w256", 2, [128, 256], F32)
    o_r = Ring(nc, "o256", 3, [128, 256], F32)
    sqr = Ring(nc, "sq", 3, [128, 256], BF16)
    pendB = []
    ucount = [0]
    BDEF = 10

    def flushB(force=False):
        while pendB and (force or pendB[0][0] <= ucount[0]):
            pendB.pop(0)[1]()
    colr = Ring(nc, "col", 8, [128, 4], F32)
    xcbr = Ring(nc, "xcb", 2, [128, 512], BF16)
    hcar = A("hcar", [128, 4], F32)
    scr = A("scr", [128, 4], F32)
    dgw = A("dgw", [128, 4, 128], BF16)
    tXL, tSGL, tHC, tDG = [Tok() for _ in range(4)]
    psr = Ring(nc, "ps", 4, [128, 512], F32, psum=True)
    pso = Ring(nc, "pso", 2, [128, 512], F32, psum=True)
    pss = Ring(nc, "pss", 2, [128, 512], F32, psum=True)

    class BRing:
        def __init__(self, rings):
            self.items = [(t, k) for r in rings for t, k in zip(r.tiles, r.toks)]
            self.i = 0

        def next(self):
            it = self.items[self.i % len(self.items)]
            self.i += 1
            return it
    bring = BRing([psr, pso, pss])

    op = Sc.op

    op("pool", lambda e: e.memset(ident_f[:], 0.0), writes=[tC])
    op("pool", lambda e: e.affine_select(out=ident_f[:], in_=ident_f[:], compare_op=ALU.not_equal, fill=1.0,
                                         base=0, pattern=[[-1, 128]], channel_multiplier=1), reads=[tC], writes=[tC])
    op("pool", lambda e: e.tensor_copy(out=ident_b[:], in_=ident_f[:]), reads=[tC], writes=[tC])
    op("pool", lambda e: e.memset(ones_b[:], 0.0), reads=[tC], writes=[tC])
    op("pool", lambda e: e.affine_select(out=maskb[:], in_=ones_b[:], compare_op=ALU.is_ge, fill=-30000.0,
                                         base=0, pattern=[[1, 128]], channel_multiplier=-1), reads=[tC], writes=[tC])
    op("pool", lambda e: e.memset(ones_b[:], 1.0), reads=[tC], writes=[tC])
    op("pool", lambda e: e.memset(ones_f[:], 1.0), reads=[tC], writes=[tC])

    def layer_start_region():
        op("pool", lambda e: e.memset(KT0[64:128, :], 0.0), writes=[tQ, tK, tV, tSG, tXL, tSGL] + wout_tok)
        op("pool", lambda e: e.memset(KT1[0:64, :], 0.0), writes=[tK])
        op("pool", lambda e: e.memset(xl[:, 0:4], 0.0), writes=[tXL])

    def load_wg(l):
        hw_ = NC * 128
        Sc.dma("pool", lambda e: e.dma_start(out=wg[:, 0:hw_], in_=wg_d[l, :, 0:hw_]), writes=[wg_tok])
        Sc.dma("pool", lambda e: e.dma_start(out=wg[:, hw_:2 * hw_], in_=wg_d[l, :, hw_:2 * hw_]), writes=[wg_tok])

    def load_layer_params(gi, l):
        par = gi % 2
        P, tP = ppt[par], pp_tok[par]
        Dr, tD = der[par], der_tok[par]
        Sc.dma("sp", lambda e: e.dma_start(out=P[:], in_=pp_d[l]), writes=[tP])
        lv = PPL["lamv"]
        ch = PPL["chan"]
        tmp, ttmp = w256.next()
        op("dve", lambda e: e.tensor_tensor(out=tmp[:, 0:64], in0=P[:, lv:lv + 64], in1=P[:, lv + 64:lv + 128], op=ALU.mult),
           reads=[tP], writes=[ttmp])
        op("dve", lambda e: e.tensor_tensor(out=tmp[:, 64:128], in0=P[:, lv + 128:lv + 192], in1=P[:, lv + 192:lv + 256], op=ALU.mult),
           reads=[tP, ttmp], writes=[ttmp])
        op("dve", lambda e: e.reduce_sum(out=Dr[:, 0:1], in_=tmp[:, 0:64], axis=mybir.AxisListType.X), reads=[ttmp], writes=[tD])
        op("dve", lambda e: e.reduce_sum(out=Dr[:, 1:2], in_=tmp[:, 64:128], axis=mybir.AxisListType.X), reads=[ttmp, tD], writes=[tD])
        op("act", lambda e: e.activation(out=Dr[:, 2:4], in_=Dr[:, 0:2], func=AF.Exp), reads=[tD], writes=[tD])
        li = lambda_init_for(l)
        op("dve", lambda e: e.scalar_tensor_tensor(out=Dr[:, 4:5], in0=Dr[:, 3:4], scalar=-li, in1=Dr[:, 2:3],
                                                   op0=ALU.add, op1=ALU.subtract), reads=[tD], writes=[tD])
        sgc = PPL["subg"]
        op("dve", lambda e: e.tensor_scalar(out=Dr[:, 5:6], in0=P[:, sgc:sgc + 1], scalar1=(1.0 - li), scalar2=None, op0=ALU.mult),
           reads=[tP, tD], writes=[tD])
        c0 = 8
        op("dve", lambda e: e.tensor_scalar(out=Dr[:, c0:c0 + 2 * NC], in0=P[:, ch + 5 * NC:ch + 7 * NC], scalar1=-1.0, scalar2=None,
                                            op0=ALU.mult), reads=[tP, tD], writes=[tD])
        op("act", lambda e: e.activation(out=Dr[:, c0 + 3 * NC:c0 + 4 * NC], in_=P[:, ch + 7 * NC:ch + 8 * NC], func=AF.Exp, scale=-1.0),
           reads=[tP, tD], writes=[tD])
        op("act", lambda e: e.activation(out=Dr[:, c0 + 3 * NC:c0 + 4 * NC], in_=Dr[:, c0 + 3 * NC:c0 + 4 * NC], func=AF.Ln, bias=1.0),
           reads=[tD], writes=[tD])
        op("dve", lambda e: e.tensor_scalar(out=Dr[:, c0 + 2 * NC:c0 + 3 * NC], in0=Dr[:, c0 + 3 * NC:c0 + 4 * NC], scalar1=-LRU_C,
                                            scalar2=None, op0=ALU.mult), reads=[tD], writes=[tD])

    tREG = Tok()

    def load_wout(l):
        op("pool", lambda e: e.memset(scr[:], 0.0), writes=[tREG, tQ, tK, tV, tSG, tXL, tSGL])
        for fc in range(FC):
            Sc.dma("pool", (lambda fc: lambda e: e.dma_start(out=wout[:, fc, :], in_=wout_d[l, fc]))(fc),
                   reads=[tREG], writes=[wout_tok[fc]])

    def load_gpost(l):
        Sc.dma("sp", lambda e: e.dma_start(out=gpost[:], in_=gpost_d[l]), writes=[tGPOST])

    def load_slab(l, cc, slot):
        t, tk = slab_t[slot], slab_k[slot]
        Sc.dma("pool", lambda e: e.dma_start(out=t[:].rearrange("p a b -> p (a b)"), in_=win_d[l, cc]), writes=[tk])
        return t, tk

    def prenorm_block(gi, xt, txt, b, par):
        col, tcol = colr.next()
        junk, tj = jk_v[par]
        op("act", lambda e: e.activation(out=junk, in_=xt, func=AF.Square, accum_out=col[:, 0:1]),
           reads=txt, writes=tj + [tcol])
        yield
        op("act", lambda e: e.activation(out=col[:, 1:2], in_=col[:, 0:1], func=AF.Ln, scale=1.0 / D, bias=NORM_EPS),
           reads=[tcol], writes=[tcol])
        yield
        op("act", lambda e: e.activation(out=col[:, 2:3], in_=col[:, 1:2], func=AF.Exp, scale=-0.5), reads=[tcol], writes=[tcol])
        yield
        hb, thb = hb_v[par]
        P, tP = ppt[gi % 2], pp_tok[gi % 2]
        gc = PPL["gprec"]
        op("dve", lambda e: e.tensor_scalar(out=hb, in0=xt, scalar1=col[:, 2:3], scalar2=None, op0=ALU.mult),
           reads=txt + [tcol], writes=thb)
        yield
        ps, tps = bring.next()
        psb = ps[:].bitcast(BF16)

        def tr(e):
            inst = None
            for i in range(DC):
                inst = e.transpose(psb[:, i * 128:(i + 1) * 128], hb[:, i * 128:(i + 1) * 128], ident_b[:])
            return inst
        op("pe", tr, reads=thb + [tC], writes=[tps])
        def ev(e):
            inst = None
            for i in range(DC):
                inst = e.tensor_scalar(out=hT[:, i, b * 128:(b + 1) * 128], in0=psb[:, i * 128:(i + 1) * 128],
                                       scalar1=P[:, gc + i:gc + i + 1], scalar2=None, op0=ALU.mult)
            return inst
        op("dve", ev, reads=[tps, tP], writes=[hT_tok[b]])
        yield

    def proj_fm(sl, tsl, evac):
        for tt in range(TT):
            ps, tps = psr.next()

            def mm(e, tt=tt, ps=ps):
                inst = None
                for dc in range(DC):
                    inst = e.matmul(ps[:, :], lhsT=sl[:, dc, :], rhs=hT[:, dc, tt * 512:(tt + 1) * 512],
                                    start=(dc == 0), stop=(dc == DC - 1))
                return inst
            op("pe", mm, reads=[tsl] + hT_tok[tt * 4:(tt + 1) * 4], writes=[tps])
            evac(tt, ps, tps)
            yield

    def silu_evac(dest, tdest):
        def evac(tt, ps, tps):
            E, tE = sil_r.next()
            G, tG = sil_r.next()
            op("act", lambda e: e.activation(out=E[:], in_=ps[:], func=AF.Exp, scale=-1.0), reads=[tps], writes=[tE])
            op("dve", lambda e: e.tensor_copy(out=G[:], in_=ps[:]), reads=[tps], writes=[tG])
            op("act", lambda e: e.activation(out=E[:], in_=E[:], func=AF.Ln, bias=1.0), reads=[tE], writes=[tE])
            op("act", lambda e: e.activation(out=E[:], in_=E[:], func=AF.Exp, scale=-1.0), reads=[tE], writes=[tE])
            op("pool", lambda e: e.tensor_tensor(out=dest[:, tt * 512:(tt + 1) * 512], in0=G[:], in1=E[:], op=ALU.mult),
               reads=[tG, tE], writes=[tdest])
        return evac

    def run_all(g):
        for _ in g:
            pass

    def q_proj(slq, tslq, par):
        def evq(tt, ps, tps):
            op("dve", lambda e: e.tensor_copy(out=QTs[par][:, tt * 512:(tt + 1) * 512], in_=ps[:]), reads=[tps], writes=[tQs[par]])
        return proj_fm(slq, tslq, evq)

    def head_stage(gi, l, h, slabs, side=None, prefetch=None, qnext=None):
        P, tP = ppt[gi % 2], pp_tok[gi % 2]
        Dr, tD = der[gi % 2], der_tok[gi % 2]
        (slq, tslq), (slk, tslk), (slv, tslv), (slg, tslg) = slabs
        sg0 = PPL["subg"]
        li = lambda_init_for(l)

        QTt, tQh = QTs[h % 2], tQs[h % 2]
        if h == 0:
            run_all(q_proj(slq, tslq, 0))

        def evk(tt, ps, tps):
            op("act", lambda e: e.activation(out=KT0[0:64, tt * 512:(tt + 1) * 512], in_=ps[0:64, :], func=AF.Copy),
               reads=[tps], writes=[tK])
            op("act", lambda e: e.activation(out=KT1[64:128, tt * 512:(tt + 1) * 512], in_=ps[64:128, :], func=AF.Copy),
               reads=[tps], writes=[tK])
        run_all(proj_fm(slk, tslk, evk))
        flushB(force=True)
        run_all(proj_fm(slg, tslg, silu_evac(sgT, tSG)))
        for k4 in range(TB // 4):
            ps, tps = psr.next()

            def mmv(e, k4=k4, ps=ps):
                inst = None
                for i in range(4):
                    kb = k4 * 4 + i
                    for dc in range(DC):
                        inst = e.matmul(ps[:, i * 128:(i + 1) * 128], lhsT=hT[:, dc, kb * 128:(kb + 1) * 128], rhs=slv[:, dc, :],
                                        start=(dc == 0), stop=(dc == DC - 1))
                return inst
            op("pe", mmv, reads=[tslv] + hT_tok[k4 * 4:(k4 + 1) * 4], writes=[tps])
            op("dve", lambda e, k4=k4, ps=ps: e.tensor_copy(out=Vt[:, k4 * 4:(k4 + 1) * 4, :],
                                                            in_=ps[:, :].rearrange("p (a c) -> p a c", a=4)),
               reads=[tps], writes=[tV])

        if prefetch is not None:
            prefetch()
        units = [(j, kb) for j in range(QT_N) for kb in range(2 * j + 2)]
        LOOK = 2
        st = {}

        def emit_qk(u):
            j, kb = units[u]
            q0 = 128 if kb == 2 * j + 1 else 0
            diag = (kb - 2 * j) if kb >= 2 * j else None
            ps, tps = psr.next()

            def qk(e):
                inst = None
                for c in (0, 1):
                    KTc = KT0 if c == 0 else KT1
                    inst = e.matmul(ps[:, c * 256 + q0:(c + 1) * 256], lhsT=KTc[:, kb * 128:(kb + 1) * 128],
                                    rhs=QTt[:, j * 256 + q0:(j + 1) * 256], start=True, stop=(diag is None))
                    if diag is not None:
                        inst = e.matmul(ps[:, c * 256 + diag * 128:c * 256 + (diag + 1) * 128], lhsT=ident_b[:], rhs=maskb[:],
                                        start=False, stop=True)
                return inst
            op("pe", qk, reads=[tK, tQh, tC], writes=[tps])
            pt, tpt = pTr.next()
            if q0 == 0:
                op("act", lambda e: e.activation(out=pt[:, :], in_=ps[:, :], func=AF.Exp, scale=0.125), reads=[tps], writes=[tpt])
            else:
                op("act", lambda e: e.activation(out=pt[:, :].rearrange("p (c q) -> p c q", c=2)[:, :, 128:256],
                                                 in_=ps[:, :].rearrange("p (c q) -> p c q", c=2)[:, :, 128:256],
                                                 func=AF.Exp, scale=0.125), reads=[tps], writes=[tpt])
            st[u] = (pt, tpt, q0)

        def emit_pv(u):
            j, kb = units[u]
            pt, tpt, q0 = st.pop(u)
            if kb == 0:
                st["po"] = pso.next()
                st["pS"] = pss.next()
            po, tpo = st["po"]
            pS, tpS = st["pS"]
            first, last = (kb == 0), (kb == 2 * j + 1)

            def pv(e):
                if q0 == 0:
                    e.matmul(po[:, :], lhsT=Vt[:, kb, :], rhs=pt[:, :], start=first, stop=last)
                    return e.matmul(pS[:, :], lhsT=ones_b[:], rhs=pt[:, :], start=first, stop=last)
                inst = None
                for c in (0, 1):
                    sl_ = slice(c * 256 + 128, c * 256 + 256)
                    e.matmul(po[:, sl_], lhsT=Vt[:, kb, :], rhs=pt[:, sl_], start=first, stop=(last and c == 1))
                    inst = e.matmul(pS[:, sl_], lhsT=ones_b[:], rhs=pt[:, sl_], start=first, stop=(last and c == 1))
                return inst
            op("pe", pv, reads=[tV, tpt, tC], writes=[tpo, tpS])
            if not last:
                return
            R, tR = w512.next()
            op("dve", lambda e: e.reciprocal(out=R[:], in_=pS[:, :]), reads=[tpS], writes=[tR])
            T, tT = w512.next()
            op("dve", lambda e: e.tensor_tensor(out=T[:], in0=po[:, :], in1=R[:], op=ALU.mult), reads=[tpo, tR], writes=[tT])
            o, to = o_r.next()
            op("dve", lambda e: e.scalar_tensor_tensor(out=o[:], in0=T[:, 256:512], scalar=Dr[:, 4:5], in1=T[:, 0:256],
                                                       op0=ALU.mult, op1=ALU.add), reads=[tT, tD], writes=[to])
            sq, tsq = sqr.next()
            op("pool", lambda e: e.tensor_tensor(out=sq[:], in0=o[:], in1=o[:], op=ALU.mult), reads=[to], writes=[tsq])

            def partB():
                psm, tpsm = psr.next()
                op("pe", lambda e: e.matmul(psm[:, 0:256], lhsT=ones_b[:], rhs=sq[:], start=True, stop=True),
                   reads=[tsq, tC], writes=[tpsm])
                rs, trs = w256.next()
                op("act", lambda e: e.activation(out=rs[:], in_=psm[:, 0:256], func=AF.Ln, scale=1.0 / 128, bias=NORM_EPS),
                   reads=[tpsm], writes=[trs])
                op("act", lambda e: e.activation(out=rs[:], in_=rs[:], func=AF.Exp, scale=-0.5), reads=[trs], writes=[trs])
                y, ty = w256.next()
                op("dve", lambda e: e.scalar_tensor_tensor(out=y[:], in0=o[:], scalar=Dr[:, 5:6], in1=rs[:],
                                                           op0=ALU.mult, op1=ALU.mult), reads=[to, tD, trs], writes=[ty])
                op("pool", lambda e: e.tensor_tensor(out=yT[:, h, j * 256:(j + 1) * 256], in0=y[:], in1=sgT[:, j * 256:(j + 1) * 256],
                                                     op=ALU.mult), reads=[ty, tSG], writes=[yT_tok[h]])
            assert len(pendB) < 3
            pendB.append((ucount[0] + BDEF, partB))

        nU = len(units)
        side3 = qnext() if qnext is not None else None
        q_every = max(1, nU // (TT + 1))
        for u in range(min(LOOK, nU)):
            emit_qk(u)
        for u in range(nU):
            if u + LOOK < nU:
                emit_qk(u + LOOK)
            emit_pv(u)
            ucount[0] += 1
            flushB()
            if side is not None:
                next(side, None)
            if side3 is not None and (u + 1) % q_every == 0:
                next(side3, None)
        if side is not None:
            run_all(side)
        if side3 is not None:
            run_all(side3)

    def lru_stage(gi, l, n, slabs):
        P, tP = ppt[gi % 2], pp_tok[gi % 2]
        Dr, tD = der[gi % 2], der_tok[gi % 2]
        (slx, tslx), (slg2, tslg2) = slabs
        ch = PPL["chan"]
        c0 = 8
        for jj in range(4):
            op("dve", lambda e, jj=jj: e.tensor_scalar(out=dgw[:, jj, :], in0=ident_f[:], scalar1=P[:, ch + jj * NC + n:ch + jj * NC + n + 1],
                                                       scalar2=None, op0=ALU.mult), reads=[tP, tC], writes=[tDG])

        def evx(tt, ps, tps):
            op("dve", lambda e: e.tensor_copy(out=xl[:, 4 + tt * 512:4 + (tt + 1) * 512], in_=ps[:]), reads=[tps], writes=[tXL])
        yield from proj_fm(slx, tslx, evx)
        yield from proj_fm(slg2, tslg2, silu_evac(sgl, tSGL))
        def lru_tile(tt):
            ts = slice(tt * 512, (tt + 1) * 512)
            ps, tps = psr.next()

            def cv(e, tt=tt, ps=ps):
                inst = None
                for jj in range(4):
                    inst = e.matmul(ps[:, :], lhsT=dgw[:, jj, :], rhs=xl[:, 1 + jj + tt * 512:1 + jj + (tt + 1) * 512],
                                    start=(jj == 0), stop=(jj == 3))
                return inst
            op("pe", cv, reads=[tDG, tXL], writes=[tps])
            xcf, tXCF = lru_r.next()
            xcb, tXCB = xcbr.next()
            op("act", lambda e, ps=ps, xcf=xcf: e.activation(out=xcf[:], in_=ps[:, :], func=AF.Identity,
                                                             bias=P[:, ch + 4 * NC + n:ch + 4 * NC + n + 1]),
               reads=[tps, tP], writes=[tXCF])
            yield
            op("dve", lambda e, xcf=xcf, xcb=xcb: e.tensor_copy(out=xcb[:], in_=xcf[:]), reads=[tXCF], writes=[tXCB])
            yield
            yield
            pr, tpr = psr.next()
            op("pe", lambda e, pr=pr, xcb=xcb: e.matmul(pr[:, :], lhsT=wg[:, n * 128:(n + 1) * 128], rhs=xcb[:], start=True, stop=True),
               reads=[wg_tok, tXCB], writes=[tpr])
            pi, tpi = psr.next()
            op("pe", lambda e, pi=pi, xcb=xcb: e.matmul(pi[:, :], lhsT=wg[:, (NC + n) * 128:(NC + n + 1) * 128], rhs=xcb[:],
                                                        start=True, stop=True), reads=[wg_tok, tXCB], writes=[tpi])
            t1, tt1 = lru_r.next()
            t2, tt2 = lru_r.next()
            t3, tt3 = lru_r.next()
            a, ta = t1, tt1
            nbr = Dr[:, c0 + n:c0 + n + 1]
            nbi = Dr[:, c0 + NC + n:c0 + NC + n + 1]
            cneg = Dr[:, c0 + 2 * NC + n:c0 + 2 * NC + n + 1]
            op("act", lambda e, pr=pr: e.activation(out=t1[:], in_=pr[:, :], func=AF.Exp, scale=-1.0, bias=nbr), reads=[tpr, tD], writes=[tt1])
            op("act", lambda e, pi=pi: e.activation(out=t3[:], in_=pi[:, :], func=AF.Exp, scale=-1.0, bias=nbi), reads=[tpi, tD], writes=[tt3])
            yield
            op("act", lambda e: e.activation(out=t1[:], in_=t1[:], func=AF.Ln, bias=1.0), reads=[tt1], writes=[tt1])
            op("act", lambda e: e.activation(out=t3[:], in_=t3[:], func=AF.Ln, bias=1.0), reads=[tt3], writes=[tt3])
            yield
            op("act", lambda e: e.activation(out=t1[:], in_=t1[:], func=AF.Exp, scale=-1.0), reads=[tt1], writes=[tt1])
            yield
            op("act", lambda e: e.activation(out=a[:], in_=t1[:], func=AF.Exp, scale=cneg), reads=[tt1, tD], writes=[ta])
            yield
            op("pool", lambda e: e.tensor_tensor(out=t2[:], in0=a[:], in1=a[:], op=ALU.mult), reads=[ta], writes=[tt2])
            yield
            op("act", lambda e: e.activation(out=t2[:], in_=t2[:], func=AF.Ln, scale=-1.0, bias=1.0000001), reads=[tt2], writes=[tt2])
            yield
            op("dve", lambda e: e.scalar_tensor_tensor(out=t2[:], in0=t2[:], scalar=0.5, in1=t3[:], op0=ALU.mult, op1=ALU.subtract),
               reads=[tt2, tt3], writes=[tt2])
            yield
            op("act", lambda e: e.activation(out=t2[:], in_=t2[:], func=AF.Exp), reads=[tt2], writes=[tt2])
            yield
            op("pool", lambda e, xcf=xcf: e.tensor_tensor(out=t2[:], in0=t2[:], in1=xcf[:], op=ALU.mult), reads=[tt2, tXCF], writes=[tt2])
            yield
            hs, tHS = t3, tt3
            init = 0.0 if tt == 0 else hcar[:, (tt - 1) % 4:(tt - 1) % 4 + 1]
            op("dve", lambda e, hs=hs, init=init: e.tensor_tensor_scan(out=hs[:], data0=a[:], data1=t2[:], initial=init,
                                                                       op0=ALU.mult, op1=ALU.add), reads=[ta, tt2, tHC], writes=[tHS])
            op("dve", lambda e, hs=hs, tt=tt: e.tensor_copy(out=hcar[:, tt % 4:tt % 4 + 1], in_=hs[:, 511:512]), reads=[tHS], writes=[tHC])
            op("pool", lambda e, ts=ts, hs=hs: e.tensor_tensor(out=yT[:, NH + n, ts], in0=hs[:], in1=sgl[:, ts], op=ALU.mult),
               reads=[tHS, tSGL], writes=[yT_tok[NH + n]])

        for t0 in range(0, TT, 2):
            gens = [lru_tile(tt) for tt in range(t0, min(TT, t0 + 2))]
            while gens:
                for g in list(gens):
                    try:
                        next(g)
                    except StopIteration:
                        gens.remove(g)
                        continue
                    yield

    DW = min(512, D)
    NHALF = D // DW
    xs_tok = [[Tok() for _ in range(TB)] for _ in range(NSEQ)]

    def boundary_block(gi, s, l, b):
        par = b % BW
        pss_ = [bring.next() for _ in range(NHALF)]

        def mmo(e):
            inst = None
            for hf in range(NHALF):
                for fc in range(FC):
                    inst = e.matmul(pss_[hf][0][:, 0:DW], lhsT=yT[:, fc, b * 128:(b + 1) * 128], rhs=wout[:, fc, hf * DW:(hf + 1) * DW],
                                    start=(fc == 0), stop=(fc == FC - 1))
            return inst
        op("pe", mmo, reads=yT_tok + wout_tok, writes=[t for _, t in pss_])
        xt, txt = xt_v[par]
        src = x_d if l == 0 else xs_d
        dst = out_d if l == L - 1 else xs_d
        rd = [xs_tok[s][b]] if l > 0 else []
        Sc.dma("sp", lambda e: e.dma_start(out=xt, in_=src[s, b * 128:(b + 1) * 128, :]), reads=rd, writes=txt)
        yield
        col, tcol = colr.next()
        junk, tj = jk_v[par]
        for hf in range(NHALF):
            op("act", lambda e, hf=hf: e.activation(out=junk[:, hf * DW:(hf + 1) * DW], in_=pss_[hf][0][:, 0:DW], func=AF.Square,
                                                    accum_out=col[:, hf:hf + 1]), reads=[pss_[hf][1]], writes=tj + [tcol])
        yield
        if NHALF == 2:
            op("dve", lambda e: e.tensor_tensor(out=col[:, 0:1], in0=col[:, 0:1], in1=col[:, 1:2], op=ALU.add), reads=[tcol], writes=[tcol])
            yield
        op("act", lambda e: e.activation(out=col[:, 2:3], in_=col[:, 0:1], func=AF.Ln, scale=1.0 / D, bias=NORM_EPS),
           reads=[tcol], writes=[tcol])
        yield
        op("act", lambda e: e.activation(out=col[:, 3:4], in_=col[:, 2:3], func=AF.Exp, scale=-0.5), reads=[tcol], writes=[tcol])
        yield
        tm, ttm = tm_v[par]
        for hf in range(NHALF):
            op("dve", lambda e, hf=hf: e.scalar_tensor_tensor(out=tm[:, hf * DW:(hf + 1) * DW], in0=pss_[hf][0][:, 0:DW], scalar=col[:, 3:4],
                                                              in1=gpost[:, hf * DW:(hf + 1) * DW], op0=ALU.mult, op1=ALU.mult),
               reads=[pss_[hf][1], tcol, tGPOST], writes=ttm)
        yield
        op("dve", lambda e: e.tensor_tensor(out=xt, in0=xt, in1=tm, op=ALU.add), reads=txt + ttm, writes=txt)
        yield
        Sc.dma("sp", lambda e: e.dma_start(out=dst[s, b * 128:(b + 1) * 128, :], in_=xt), reads=txt,
               writes=([xs_tok[s][b]] if l < L - 1 else []))
        if l < L - 1:
            yield from prenorm_block(gi + 1, xt, txt, b, par)

    def interleave(gens, width=2, stagger=0):
        gens = list(gens)
        active = []
        while gens or active:
            if gens and len(active) < width and (not active or active[-1][1] >= stagger):
                active.append([gens.pop(0), 0])
            for it in list(active):
                try:
                    next(it[0])
                    it[1] += 1
                except StopIteration:
                    active.remove(it)

    assert NH == NC
    pairs = [(s_, l_, i_) for s_ in range(NSEQ) for l_ in range(L) for i_ in range(NH)]
    pair_slabs = {}

    def load_pair(k):
        if k >= len(pairs):
            return
        _, l_, i_ = pairs[k]
        lo = 4 + 2 * (k % 2)
        pair_slabs[k] = ([load_slab(l_, i_, 0), load_slab(l_, NH + i_, 1), load_slab(l_, 2 * NH + i_, 2), load_slab(l_, 3 * NH + i_, 3)],
                         [load_slab(l_, 4 * NH + i_, lo), load_slab(l_, 4 * NH + NC + i_, lo + 1)])

    load_pair(0)
    k = 0
    for s in range(NSEQ):
        gi0 = s * L
        load_layer_params(gi0, 0)

        def first_pre(b, s=s):
            xt, txt = xt_v[b % BW]
            Sc.dma("sp", lambda e: e.dma_start(out=xt, in_=x_d[s, b * 128:(b + 1) * 128, :]), writes=txt)
            yield
            yield from prenorm_block(gi0, xt, txt, b, b % BW)
        interleave([first_pre(b) for b in range(TB)], width=BW, stagger=2)
        for l in range(L):
            gi = gi0 + l
            layer_start_region()
            load_wg(l)
            load_gpost(l)
            if l + 1 < L:
                load_layer_params(gi + 1, l + 1)
            for i in range(NH):
                hsl, lsl = pair_slabs.pop(k)
                side = lru_stage(gi, l, i, lsl)
                qn = None
                if i + 1 < NH:
                    qn = (lambda k=k, i=i: q_proj(pair_slabs[k + 1][0][0][0], pair_slabs[k + 1][0][0][1], (i + 1) % 2))
                head_stage(gi, l, i, hsl, side=side, prefetch=(lambda k=k: load_pair(k + 1)), qnext=qn)
                k += 1
            flushB(force=True)
            load_wout(l)
            interleave([boundary_block(gi, s, l, b) for b in range(TB)], width=BW, stagger=4)

    if dbg:
        pass
    with ExitStack() as stack:
        Sc.emit(nc, stack)
    return nc


def pack_weights(inp, L, D, NH, NC):
    DC = D // 128
    W = NH * 128
    WL = NC * 128
    w_in = np.asarray(inp["w_in"], np.float32)
    ncc = 4 * NH + 2 * NC
    w_in_r = np.ascontiguousarray(w_in.reshape(L, DC, 128, ncc, 128).transpose(0, 3, 2, 1, 4)).reshape(L, ncc, 128, DC * 128)
    w_out = np.asarray(inp["w_out"], np.float32)
    w_out_r = np.ascontiguousarray(w_out.reshape(L, NH + NC, 128, D))
    wr = np.asarray(inp["w_rgate"], np.float32).transpose(0, 2, 1, 3).reshape(L, 128, NC * 128)
    wi = np.asarray(inp["w_igate"], np.float32).transpose(0, 2, 1, 3).reshape(L, 128, NC * 128)
    w_g = np.ascontiguousarray(np.concatenate([wr, wi], axis=2))
    PPL = pp_layout(D)
    pp = np.zeros((L, 128, PPL["_n"]), np.float32)
    pp_gpre = np.asarray(inp["pre_norm_g"], np.float32).reshape(L, DC, 128).transpose(0, 2, 1)
    gpost = np.ascontiguousarray(np.broadcast_to(np.asarray(inp["post_norm_g"], np.float32)[:, None, :], (L, 128, D)))
    lv = PPL["lamv"]
    for i, k in enumerate(("lambda_q1", "lambda_k1", "lambda_q2", "lambda_k2")):
        pp[:, :, lv + 64 * i:lv + 64 * (i + 1)] = np.asarray(inp[k], np.float32)[:, None, :]
    pp[:, :, PPL["subg"]] = np.asarray(inp["subln_g"], np.float32)
    ch = PPL["chan"]
    cw = np.asarray(inp["conv_w"], np.float32)
    chan = [cw[:, 0], cw[:, 1], cw[:, 2], cw[:, 3], inp["conv_b"], inp["b_rgate"], inp["b_igate"], inp["lru_lambda"]]
    for k, v in enumerate(chan):
        v = np.asarray(v, np.float32).reshape(L, NC, 128).transpose(0, 2, 1)
        pp[:, :, ch + k * NC:ch + (k + 1) * NC] = v
    pp[:, :, PPL["gprec"]:PPL["gprec"] + DC] = pp_gpre
    return {"w_in": w_in_r, "w_out": w_out_r, "w_g": w_g, "pp": pp, "gpost": gpost}


_NC_CACHE = {}


def kernel(**inputs):
    L, D, NH, NC, S = 4, 1024, 8, 8, 2048
    x = np.asarray(inputs["x"], np.float32)
    B = x.shape[0]
    n_cores = 8
    nseq = B // n_cores
    wts = pack_weights(inputs, L, D, NH, NC)
    key = (L, nseq, S, D, NH, NC)
    if key not in _NC_CACHE:
        _NC_CACHE[key] = build_program(L=L, NSEQ=nseq, S=S, D=D, NH=NH, NC=NC)
    nc = _NC_CACHE[key]
    in_maps = []
    for c in range(n_cores):
        m = {"x": np.ascontiguousarray(x[c * nseq:(c + 1) * nseq])}
        m.update(wts)
        in_maps.append(m)
    res = run_bass_kernel_spmd(nc, in_maps, core_ids=list(range(n_cores)))
    return np.concatenate([np.asarray(r["out"], np.float32) for r in res.results], axis=0)
```

```python
import math
from contextlib import ExitStack

import numpy as np
import concourse.bass as bass
import concourse.mybir as mybir
from concourse.bass_utils import run_bass_kernel_spmd

F32 = mybir.dt.float32
BF16 = mybir.dt.bfloat16
AF = mybir.ActivationFunctionType
ALU = mybir.AluOpType

ENGS = ("pe", "act", "dve", "pool", "sp")
SYNC_SAME = ("act", "dve", "pool")
NORM_EPS = 1e-6
LRU_C = 8.0


class Tok:
    __slots__ = ("w", "rs_eng", "rs_dma", "excl")

    def __init__(self, excl=False):
        self.w = None
        self.rs_eng = {}
        self.rs_dma = []
        self.excl = excl


class Op:
    __slots__ = ("eng", "fn", "deps", "need_inc", "idx", "is_dma", "dsem", "dval", "cnt")

    def __init__(self, eng, fn, is_dma=False):
        self.eng = eng
        self.fn = fn
        self.deps = []
        self.need_inc = is_dma
        self.idx = -1
        self.is_dma = is_dma
        self.dsem = None
        self.dval = 0
        self.cnt = 0


class Sched:
    def __init__(self, n_dma_sems=8):
        self.ops = {e: [] for e in ENGS}
        self.n_dma_sems = n_dma_sems

    def _add(self, o, reads, writes):
        ex = [t for t in reads if t.excl and t not in writes]
        if ex:
            reads = [t for t in reads if not t.excl]
            writes = list(writes) + ex
        deps = {}

        def add(p):
            if p is not None and p is not o:
                deps[id(p)] = p

        for t in reads:
            add(t.w)
        for t in writes:
            add(t.w)
            for p in t.rs_eng.values():
                add(p)
            for p in t.rs_dma:
                add(p)
        best = {}
        dl = []
        for p in deps.values():
            if p.is_dma:
                dl.append(p)
            else:
                b = best.get(p.eng)
                if b is None or p.idx > b.idx:
                    best[p.eng] = p
        for e, p in best.items():
            if e == o.eng and not o.is_dma and e not in SYNC_SAME:
                continue
            dl.append(p)
        o.deps = dl
        for p in dl:
            p.need_inc = True
        for t in reads:
            if o.is_dma:
                t.rs_dma.append(o)
            else:
                t.rs_eng[o.eng] = o
        for t in writes:
            t.w = o
            t.rs_eng = {}
            t.rs_dma = []
        o.idx = len(self.ops[o.eng])
        self.ops[o.eng].append(o)
        return o

    def op(self, eng, fn, reads=(), writes=()):
        return self._add(Op(eng, fn), reads, writes)

    def dma(self, eng, fn, reads=(), writes=()):
        return self._add(Op(eng, fn, is_dma=True), reads, writes)

    def emit(self, nc, stack):
        esem = {e: stack.enter_context(nc.semaphore("s_" + e)) for e in ENGS}
        dsems = {}
        for e in ENGS:
            if any(o.is_dma for o in self.ops[e]):
                dsems[e] = [stack.enter_context(nc.semaphore("d_%s%d" % (e, i)))
                            for i in range(self.n_dma_sems)]
        for e in ENGS:
            c = 0
            tot = [0] * self.n_dma_sems
            rr = 0
            for o in self.ops[e]:
                if o.is_dma:
                    k = rr % self.n_dma_sems
                    rr += 1
                    o.dsem = (e, k)
                    o.cnt = tot[k]
                    tot[k] += 16
                    o.dval = tot[k]
                else:
                    if o.need_inc:
                        c += 1
                    o.cnt = c

        def ev(p):
            if p.is_dma:
                return dsems[p.dsem[0]][p.dsem[1]], ("d", p.dsem), p.dval
            return esem[p.eng], ("e", p.eng), p.cnt

        def run(e, eng):
            waited = {}
            for o in self.ops[e]:
                ws = []
                if o.is_dma and o.cnt > 0:
                    ws.append((dsems[e][o.dsem[1]], ("d", o.dsem), o.cnt))
                for p in o.deps:
                    ws.append(ev(p))
                for sem, key, val in ws:
                    if waited.get(key, 0) >= val:
                        continue
                    waited[key] = val
                    eng.wait_ge(sem, val)
                inst = o.fn(eng)
                if o.is_dma:
                    inst.then_inc(dsems[e][o.dsem[1]], 16)
                elif o.need_inc:
                    inst.then_inc(esem[e], 1)
            if e in dsems:
                tot = {}
                for o in self.ops[e]:
                    if o.is_dma:
                        tot[o.dsem] = max(tot.get(o.dsem, 0), o.dval)
                for (qe, k), v in tot.items():
                    if waited.get(("d", (qe, k)), 0) < v:
                        eng.wait_ge(dsems[qe][k], v)

        block = stack.enter_context(nc.Block())

        @block.tensor
        def _(eng):
            run("pe", eng)

        @block.scalar
        def _(eng):
            run("act", eng)

        @block.vector
        def _(eng):
            run("dve", eng)

        @block.gpsimd
        def _(eng):
            run("pool", eng)

        @block.sync
        def _(eng):
            run("sp", eng)


class Ring:
    def __init__(self, nc, name, n, shape, dtype, psum=False):
        alloc = nc.alloc_psum_tensor if psum else nc.alloc_sbuf_tensor
        self.tiles = [alloc("%s%d" % (name, i), shape, dtype) for i in range(n)]
        self.toks = [Tok(excl=psum) for _ in range(n)]
        self.i = 0

    def next(self):
        k = self.i % len(self.tiles)
        self.i += 1
        return self.tiles[k], self.toks[k]


def lambda_init_for(layer_idx):
    return 0.8 - 0.6 * math.exp(-0.3 * layer_idx)


def pp_layout(D):
    o = {}
    c = 0
    for name, n in (("lamv", 256), ("subg", 1), ("chan", 64), ("gprec", 8)):
        o[name] = c
        c += n
    o["_n"] = c
    return o


def build_program(L=4, NSEQ=2, S=2048, D=1024, NH=8, NC=8, dbg=False):
    nc = bass.Bass("TRN2", target_bir_lowering=False)
    DC = D // 128
    TB = S // 128
    TT = S // 512
    QT_N = S // 256
    NCC = 4 * NH + 2 * NC
    FC = NH + NC
    PPL = pp_layout(D)
    KP = PPL["_n"]

    x_d = nc.dram_tensor("x", [NSEQ, S, D], F32, kind="ExternalInput").ap()
    win_d = nc.dram_tensor("w_in", [L, NCC, 128, DC * 128], F32, kind="ExternalInput").ap()
    wout_d = nc.dram_tensor("w_out", [L, FC, 128, D], F32, kind="ExternalInput").ap()
    wg_d = nc.dram_tensor("w_g", [L, 128, 2 * NC * 128], F32, kind="ExternalInput").ap()
    pp_d = nc.dram_tensor("pp", [L, 128, KP], F32, kind="ExternalInput").ap()
    gpost_d = nc.dram_tensor("gpost", [L, 128, D], F32, kind="ExternalInput").ap()
    out_d = nc.dram_tensor("out", [NSEQ, S, D], F32, kind="ExternalOutput").ap()
    xs_d = nc.dram_tensor("xs", [NSEQ, S, D], F32, kind="Internal").ap()
    if dbg:
        dbg_hT = nc.dram_tensor("dbg_hT", [128, DC * S], F32, kind="ExternalOutput").ap()
        dbg_yT = nc.dram_tensor("dbg_yT", [128, FC * S], F32, kind="ExternalOutput").ap()

    Sc = Sched()
    A = nc.alloc_sbuf_tensor

    hT = A("hT", [128, DC, S], BF16)
    hT_tok = [Tok() for _ in range(TB)]
    yT = A("yT", [128, FC, S], BF16)
    yT_tok = [Tok() for _ in range(FC)]
    BIGN = max(FC * D, 7 * S + 8)
    big = A("big", [128, BIGN], BF16)
    wout = big[:, 0:FC * D].rearrange("p (f d) -> p f d", f=FC)
    wout_tok = [Tok() for _ in range(FC)]
    gpost = A("gpost_t", [128, D], F32)
    tGPOST = Tok()
    wg = A("wg", [128, 2 * NC * 128], BF16)
    wg_tok = Tok()
    ppt = [A("pp%d" % i, [128, KP], F32) for i in range(2)]
    pp_tok = [Tok() for _ in range(2)]
    der = [A("der%d" % i, [128, 16 + 4 * NC], F32) for i in range(2)]
    der_tok = [Tok() for _ in range(2)]
    slab_t = [A("slab%d" % i, [128, DC, 128], BF16) for i in range(8)]
    slab_k = [Tok() for _ in range(8)]
    QTs = [big[:, 0:S], A("QTb", [128, S], BF16)]
    KT0 = big[:, S:2 * S]
    KT1 = big[:, 2 * S:3 * S]
    Vt = big[:, 3 * S:4 * S].rearrange("p (a c) -> p a c", a=TB)
    sgT = big[:, 4 * S:5 * S]
    sgl = big[:, 5 * S:6 * S]
    xl = big[:, 6 * S:7 * S + 4]
    tQs = [Tok(), Tok()]
    tQ = tQs[0]
    tK, tV, tSG = Tok(), Tok(), Tok()
    pTr = Ring(nc, "pT", 4, [128, 512], BF16)
    ident_f = A("ident_f", [128, 128], F32)
    ident_b = A("ident_b", [128, 128], BF16)
    maskb = A("maskb", [128, 128], BF16)
    ones_b = A("ones_b", [128, 128], BF16)
    ones_f = A("ones_f", [1, 128], F32)
    tC = Tok()
    NW = 16
    WR = A("wr", [128, NW * 512], F32)
    wtok = [Tok() for _ in range(NW)]

    class VRing:
        def __init__(self, idx):
            self.idx = idx
            self.i = 0

        def next(self):
            k = self.idx[self.i % len(self.idx)]
            self.i += 1
            return WR[:, k * 512:(k + 1) * 512], wtok[k]
    lru_r = VRing(list(range(0, 8)))
    w512 = VRing(list(range(8, 12)))
    sil_r = VRing(list(range(12, 16)))
    assert D <= 1024
    BW = 3
    xt_v = [(WR[:, (5 * i) * 512:(5 * i) * 512 + D], wtok[5 * i:5 * i + 2]) for i in range(BW)]
    tm_v = [(WR[:, (5 * i + 2) * 512:(5 * i + 2) * 512 + D], wtok[5 * i + 2:5 * i + 4]) for i in range(BW)]
    hb_v = [(WR[:, (5 * i + 4) * 512:(5 * i + 5) * 512].bitcast(BF16)[:, 0:D], wtok[5 * i + 4:5 * i + 5]) for i in range(BW)]
    junk_ap = WR[:, 15 * 512:16 * 512].bitcast(BF16)[:, 0:D]
    jk_v = [(junk_ap, [wtok[15]]) for i in range(BW)]
    w256 = Ring(nc, "w256", 2, [128, 256], F32)
    o_r = Ring(nc, "o256", 3, [128, 256], F32)
    sqr = Ring(nc, "sq", 3, [128, 256], BF16)
    pendB = []
    ucount = [0]
    BDEF = 10

    def flushB(force=False):
        while pendB and (force or pendB[0][0] <= ucount[0]):
            pendB.pop(0)[1]()
    colr = Ring(nc, "col", 8, [128, 4], F32)
    xcbr = Ring(nc, "xcb", 2, [128, 512], BF16)
    hcar = A("hcar", [128, 4], F32)
    scr = A("scr", [128, 4], F32)
    dgw = A("dgw", [128, 4, 128], BF16)
    tXL, tSGL, tHC, tDG = [Tok() for _ in range(4)]
    psr = Ring(nc, "ps", 4, [128, 512], F32, psum=True)
    pso = Ring(nc, "pso", 2, [128, 512], F32, psum=True)
    pss = Ring(nc, "pss", 2, [128, 512], F32, psum=True)

    class BRing:
        def __init__(self, rings):
            self.items = [(t, k) for r in rings for t, k in zip(r.tiles, r.toks)]
            self.i = 0

        def next(self):
            it = self.items[self.i % len(self.items)]
            self.i += 1
            return it
    bring = BRing([psr, pso, pss])

    op = Sc.op

    op("pool", lambda e: e.memset(ident_f[:], 0.0), writes=[tC])
    op("pool", lambda e: e.affine_select(out=ident_f[:], in_=ident_f[:], compare_op=ALU.not_equal, fill=1.0,
                                         base=0, pattern=[[-1, 128]], channel_multiplier=1), reads=[tC], writes=[tC])
    op("pool", lambda e: e.tensor_copy(out=ident_b[:], in_=ident_f[:]), reads=[tC], writes=[tC])
    op("pool", lambda e: e.memset(ones_b[:], 0.0), reads=[tC], writes=[tC])
    op("pool", lambda e: e.affine_select(out=maskb[:], in_=ones_b[:], compare_op=ALU.is_ge, fill=-30000.0,
                                         base=0, pattern=[[1, 128]], channel_multiplier=-1), reads=[tC], writes=[tC])
    op("pool", lambda e: e.memset(ones_b[:], 1.0), reads=[tC], writes=[tC])
    op("pool", lambda e: e.memset(ones_f[:], 1.0), reads=[tC], writes=[tC])

    def layer_start_region():
        op("pool", lambda e: e.memset(KT0[64:128, :], 0.0), writes=[tQ, tK, tV, tSG, tXL, tSGL] + wout_tok)
        op("pool", lambda e: e.memset(KT1[0:64, :], 0.0), writes=[tK])
        op("pool", lambda e: e.memset(xl[:, 0:4], 0.0), writes=[tXL])

    def load_wg(l):
        hw_ = NC * 128
        Sc.dma("pool", lambda e: e.dma_start(out=wg[:, 0:hw_], in_=wg_d[l, :, 0:hw_]), writes=[wg_tok])
        Sc.dma("pool", lambda e: e.dma_start(out=wg[:, hw_:2 * hw_], in_=wg_d[l, :, hw_:2 * hw_]), writes=[wg_tok])

    def load_layer_params(gi, l):
        par = gi % 2
        P, tP = ppt[par], pp_tok[par]
        Dr, tD = der[par], der_tok[par]
        Sc.dma("sp", lambda e: e.dma_start(out=P[:], in_=pp_d[l]), writes=[tP])
        lv = PPL["lamv"]
        ch = PPL["chan"]
        tmp, ttmp = w256.next()
        op("dve", lambda e: e.tensor_tensor(out=tmp[:, 0:64], in0=P[:, lv:lv + 64], in1=P[:, lv + 64:lv + 128], op=ALU.mult),
           reads=[tP], writes=[ttmp])
        op("dve", lambda e: e.tensor_tensor(out=tmp[:, 64:128], in0=P[:, lv + 128:lv + 192], in1=P[:, lv + 192:lv + 256], op=ALU.mult),
           reads=[tP, ttmp], writes=[ttmp])
        op("dve", lambda e: e.reduce_sum(out=Dr[:, 0:1], in_=tmp[:, 0:64], axis=mybir.AxisListType.X), reads=[ttmp], writes=[tD])
        op("dve", lambda e: e.reduce_sum(out=Dr[:, 1:2], in_=tmp[:, 64:128], axis=mybir.AxisListType.X), reads=[ttmp, tD], writes=[tD])
        op("act", lambda e: e.activation(out=Dr[:, 2:4], in_=Dr[:, 0:2], func=AF.Exp), reads=[tD], writes=[tD])
        li = lambda_init_for(l)
        op("dve", lambda e: e.scalar_tensor_tensor(out=Dr[:, 4:5], in0=Dr[:, 3:4], scalar=-li, in1=Dr[:, 2:3],
                                                   op0=ALU.add, op1=ALU.subtract), reads=[tD], writes=[tD])
        sgc = PPL["subg"]
        op("dve", lambda e: e.tensor_scalar(out=Dr[:, 5:6], in0=P[:, sgc:sgc + 1], scalar1=(1.0 - li), scalar2=None, op0=ALU.mult),
           reads=[tP, tD], writes=[tD])
        c0 = 8
        op("dve", lambda e: e.tensor_scalar(out=Dr[:, c0:c0 + 2 * NC], in0=P[:, ch + 5 * NC:ch + 7 * NC], scalar1=-1.0, scalar2=None,
                                            op0=ALU.mult), reads=[tP, tD], writes=[tD])
        op("act", lambda e: e.activation(out=Dr[:, c0 + 3 * NC:c0 + 4 * NC], in_=P[:, ch + 7 * NC:ch + 8 * NC], func=AF.Exp, scale=-1.0),
           reads=[tP, tD], writes=[tD])
        op("act", lambda e: e.activation(out=Dr[:, c0 + 3 * NC:c0 + 4 * NC], in_=Dr[:, c0 + 3 * NC:c0 + 4 * NC], func=AF.Ln, bias=1.0),
           reads=[tD], writes=[tD])
        op("dve", lambda e: e.tensor_scalar(out=Dr[:, c0 + 2 * NC:c0 + 3 * NC], in0=Dr[:, c0 + 3 * NC:c0 + 4 * NC], scalar1=-LRU_C,
                                            scalar2=None, op0=ALU.mult), reads=[tD], writes=[tD])

    tREG = Tok()

    def load_wout(l):
        op("pool", lambda e: e.memset(scr[:], 0.0), writes=[tREG, tQ, tK, tV, tSG, tXL, tSGL])
        for fc in range(FC):
            Sc.dma("pool", (lambda fc: lambda e: e.dma_start(out=wout[:, fc, :], in_=wout_d[l, fc]))(fc),
                   reads=[tREG], writes=[wout_tok[fc]])

    def load_gpost(l):
        Sc.dma("sp", lambda e: e.dma_start(out=gpost[:], in_=gpost_d[l]), writes=[tGPOST])

    def load_slab(l, cc, slot):
        t, tk = slab_t[slot], slab_k[slot]
        Sc.dma("pool", lambda e: e.dma_start(out=t[:].rearrange("p a b -> p (a b)"), in_=win_d[l, cc]), writes=[tk])
        return t, tk

    def prenorm_block(gi, xt, txt, b, par):
        col, tcol = colr.next()
        junk, tj = jk_v[par]
        op("act", lambda e: e.activation(out=junk, in_=xt, func=AF.Square, accum_out=col[:, 0:1]),
           reads=txt, writes=tj + [tcol])
        yield
        op("act", lambda e: e.activation(out=col[:, 1:2], in_=col[:, 0:1], func=AF.Ln, scale=1.0 / D, bias=NORM_EPS),
           reads=[tcol], writes=[tcol])
        yield
        op("act", lambda e: e.activation(out=col[:, 2:3], in_=col[:, 1:2], func=AF.Exp, scale=-0.5), reads=[tcol], writes=[tcol])
        yield
        hb, thb = hb_v[par]
        P, tP = ppt[gi % 2], pp_tok[gi % 2]
        gc = PPL["gprec"]
        op("dve", lambda e: e.tensor_scalar(out=hb, in0=xt, scalar1=col[:, 2:3], scalar2=None, op0=ALU.mult),
           reads=txt + [tcol], writes=thb)
        yield
        ps, tps = bring.next()
        psb = ps[:].bitcast(BF16)

        def tr(e):
            inst = None
            for i in range(DC):
                inst = e.transpose(psb[:, i * 128:(i + 1) * 128], hb[:, i * 128:(i + 1) * 128], ident_b[:])
            return inst
        op("pe", tr, reads=thb + [tC], writes=[tps])
        def ev(e):
            inst = None
            for i in range(DC):
                inst = e.tensor_scalar(out=hT[:, i, b * 128:(b + 1) * 128], in0=psb[:, i * 128:(i + 1) * 128],
                                       scalar1=P[:, gc + i:gc + i + 1], scalar2=None, op0=ALU.mult)
            return inst
        op("dve", ev, reads=[tps, tP], writes=[hT_tok[b]])
        yield

    def proj_fm(sl, tsl, evac):
        for tt in range(TT):
            ps, tps = psr.next()

            def mm(e, tt=tt, ps=ps):
                inst = None
                for dc in range(DC):
                    inst = e.matmul(ps[:, :], lhsT=sl[:, dc, :], rhs=hT[:, dc, tt * 512:(tt + 1) * 512],
                                    start=(dc == 0), stop=(dc == DC - 1))
                return inst
            op("pe", mm, reads=[tsl] + hT_tok[tt * 4:(tt + 1) * 4], writes=[tps])
            evac(tt, ps, tps)
            yield

    def silu_evac(dest, tdest):
        def evac(tt, ps, tps):
            E, tE = sil_r.next()
            G, tG = sil_r.next()
            op("act", lambda e: e.activation(out=E[:], in_=ps[:], func=AF.Exp, scale=-1.0), reads=[tps], writes=[tE])
            op("dve", lambda e: e.tensor_copy(out=G[:], in_=ps[:]), reads=[tps], writes=[tG])
            op("act", lambda e: e.activation(out=E[:], in_=E[:], func=AF.Ln, bias=1.0), reads=[tE], writes=[tE])
            op("act", lambda e: e.activation(out=E[:], in_=E[:], func=AF.Exp, scale=-1.0), reads=[tE], writes=[tE])
            op("pool", lambda e: e.tensor_tensor(out=dest[:, tt * 512:(tt + 1) * 512], in0=G[:], in1=E[:], op=ALU.mult),
               reads=[tG, tE], writes=[tdest])
        return evac

    def run_all(g):
        for _ in g:
            pass

    def q_proj(slq, tslq, par):
        def evq(tt, ps, tps):
            op("dve", lambda e: e.tensor_copy(out=QTs[par][:, tt * 512:(tt + 1) * 512], in_=ps[:]), reads=[tps], writes=[tQs[par]])
        return proj_fm(slq, tslq, evq)

    def head_stage(gi, l, h, slabs, side=None, prefetch=None, qnext=None, side1=None):
        P, tP = ppt[gi % 2], pp_tok[gi % 2]
        Dr, tD = der[gi % 2], der_tok[gi % 2]
        (slq, tslq), (slk, tslk), (slv, tslv), (slg, tslg) = slabs
        sg0 = PPL["subg"]
        li = lambda_init_for(l)

        QTt, tQh = QTs[h % 2], tQs[h % 2]
        def with_side(g):
            for _ in g:
                if side1 is not None:
                    next(side1, None)
        if h == 0:
            with_side(q_proj(slq, tslq, 0))

        def evk(tt, ps, tps):
            op("act", lambda e: e.activation(out=KT0[0:64, tt * 512:(tt + 1) * 512], in_=ps[0:64, :], func=AF.Copy),
               reads=[tps], writes=[tK])
            op("act", lambda e: e.activation(out=KT1[64:128, tt * 512:(tt + 1) * 512], in_=ps[64:128, :], func=AF.Copy),
               reads=[tps], writes=[tK])
        with_side(proj_fm(slk, tslk, evk))
        flushB(force=True)
        with_side(proj_fm(slg, tslg, silu_evac(sgT, tSG)))
        for k4 in range(TB // 4):
            ps, tps = psr.next()

            def mmv(e, k4=k4, ps=ps):
                inst = None
                for i in range(4):
                    kb = k4 * 4 + i
                    for dc in range(DC):
                        inst = e.matmul(ps[:, i * 128:(i + 1) * 128], lhsT=hT[:, dc, kb * 128:(kb + 1) * 128], rhs=slv[:, dc, :],
                                        start=(dc == 0), stop=(dc == DC - 1))
                return inst
            op("pe", mmv, reads=[tslv] + hT_tok[k4 * 4:(k4 + 1) * 4], writes=[tps])
            op("dve", lambda e, k4=k4, ps=ps: e.tensor_copy(out=Vt[:, k4 * 4:(k4 + 1) * 4, :],
                                                            in_=ps[:, :].rearrange("p (a c) -> p a c", a=4)),
               reads=[tps], writes=[tV])
            with_side([0, 1])
        if side1 is not None:
            run_all(side1)

        if prefetch is not None:
            prefetch()
        units = [(j, kb) for j in range(QT_N) for kb in range(2 * j + 2)]
        LOOK = 2
        st = {}

        def emit_qk(u):
            j, kb = units[u]
            q0 = 128 if kb == 2 * j + 1 else 0
            diag = (kb - 2 * j) if kb >= 2 * j else None
            ps, tps = psr.next()

            def qk(e):
                inst = None
                for c in (0, 1):
                    KTc = KT0 if c == 0 else KT1
                    inst = e.matmul(ps[:, c * 256 + q0:(c + 1) * 256], lhsT=KTc[:, kb * 128:(kb + 1) * 128],
                                    rhs=QTt[:, j * 256 + q0:(j + 1) * 256], start=True, stop=(diag is None))
                    if diag is not None:
                        inst = e.matmul(ps[:, c * 256 + diag * 128:c * 256 + (diag + 1) * 128], lhsT=ident_b[:], rhs=maskb[:],
                                        start=False, stop=True)
                return inst
            op("pe", qk, reads=[tK, tQh, tC], writes=[tps])
            pt, tpt = pTr.next()
            if q0 == 0:
                op("act", lambda e: e.activation(out=pt[:, :], in_=ps[:, :], func=AF.Exp, scale=0.125), reads=[tps], writes=[tpt])
            else:
                op("act", lambda e: e.activation(out=pt[:, :].rearrange("p (c q) -> p c q", c=2)[:, :, 128:256],
                                                 in_=ps[:, :].rearrange("p (c q) -> p c q", c=2)[:, :, 128:256],
                                                 func=AF.Exp, scale=0.125), reads=[tps], writes=[tpt])
            st[u] = (pt, tpt, q0)

        def emit_pv(u):
            j, kb = units[u]
            pt, tpt, q0 = st.pop(u)
            if kb == 0:
                st["po"] = pso.next()
                st["pS"] = pss.next()
            po, tpo = st["po"]
            pS, tpS = st["pS"]
            first, last = (kb == 0), (kb == 2 * j + 1)

            def pv(e):
                if q0 == 0:
                    e.matmul(po[:, :], lhsT=Vt[:, kb, :], rhs=pt[:, :], start=first, stop=last)
                    return e.matmul(pS[:, :], lhsT=ones_b[:], rhs=pt[:, :], start=first, stop=last)
                inst = None
                for c in (0, 1):
                    sl_ = slice(c * 256 + 128, c * 256 + 256)
                    e.matmul(po[:, sl_], lhsT=Vt[:, kb, :], rhs=pt[:, sl_], start=first, stop=(last and c == 1))
                    inst = e.matmul(pS[:, sl_], lhsT=ones_b[:], rhs=pt[:, sl_], start=first, stop=(last and c == 1))
                return inst
            op("pe", pv, reads=[tV, tpt, tC], writes=[tpo, tpS])
            if not last:
                return
            R, tR = w512.next()
            op("dve", lambda e: e.reciprocal(out=R[:], in_=pS[:, :]), reads=[tpS], writes=[tR])
            T, tT = w512.next()
            op("dve", lambda e: e.tensor_tensor(out=T[:], in0=po[:, :], in1=R[:], op=ALU.mult), reads=[tpo, tR], writes=[tT])
            o, to = o_r.next()
            op("dve", lambda e: e.scalar_tensor_tensor(out=o[:], in0=T[:, 256:512], scalar=Dr[:, 4:5], in1=T[:, 0:256],
                                                       op0=ALU.mult, op1=ALU.add), reads=[tT, tD], writes=[to])
            sq, tsq = sqr.next()
            op("pool", lambda e: e.tensor_tensor(out=sq[:], in0=o[:], in1=o[:], op=ALU.mult), reads=[to], writes=[tsq])

            def partB():
                psm, tpsm = psr.next()
                op("pe", lambda e: e.matmul(psm[:, 0:256], lhsT=ones_b[:], rhs=sq[:], start=True, stop=True),
                   reads=[tsq, tC], writes=[tpsm])
                rs, trs = w256.next()
                op("act", lambda e: e.activation(out=rs[:], in_=psm[:, 0:256], func=AF.Ln, scale=1.0 / 128, bias=NORM_EPS),
                   reads=[tpsm], writes=[trs])
                op("act", lambda e: e.activation(out=rs[:], in_=rs[:], func=AF.Exp, scale=-0.5), reads=[trs], writes=[trs])
                y, ty = w256.next()
                op("dve", lambda e: e.scalar_tensor_tensor(out=y[:], in0=o[:], scalar=Dr[:, 5:6], in1=rs[:],
                                                           op0=ALU.mult, op1=ALU.mult), reads=[to, tD, trs], writes=[ty])
                op("pool", lambda e: e.tensor_tensor(out=yT[:, h, j * 256:(j + 1) * 256], in0=y[:], in1=sgT[:, j * 256:(j + 1) * 256],
                                                     op=ALU.mult), reads=[ty, tSG], writes=[yT_tok[h]])
            assert len(pendB) < 3
            pendB.append((ucount[0] + BDEF, partB))

        nU = len(units)
        side3 = qnext() if qnext is not None else None
        q_every = max(1, nU // (TT + 1))
        for u in range(min(LOOK, nU)):
            emit_qk(u)
        for u in range(nU):
            if u + LOOK < nU:
                emit_qk(u + LOOK)
            emit_pv(u)
            ucount[0] += 1
            flushB()
            if side is not None:
                next(side, None)
            if side3 is not None and (u + 1) % q_every == 0:
                next(side3, None)
        if side is not None:
            run_all(side)
        if side3 is not None:
            run_all(side3)

    def lru_proj_raw(slabs):
        (slx, tslx), (slg2, tslg2) = slabs

        def evx(tt, ps, tps):
            op("dve", lambda e: e.tensor_copy(out=xl[:, 4 + tt * 512:4 + (tt + 1) * 512], in_=ps[:]), reads=[tps], writes=[tXL])

        def evg(tt, ps, tps):
            op("dve", lambda e: e.tensor_copy(out=sgl[:, tt * 512:(tt + 1) * 512], in_=ps[:]), reads=[tps], writes=[tSGL])
        yield from proj_fm(slx, tslx, evx)
        yield from proj_fm(slg2, tslg2, evg)

    def lru_silu():
        def tile(tt):
            ts = slice(tt * 512, (tt + 1) * 512)
            E, tE = lru_r.next()
            op("act", lambda e: e.activation(out=E[:], in_=sgl[:, ts], func=AF.Exp, scale=-1.0), reads=[tSGL], writes=[tE])
            yield
            op("act", lambda e: e.activation(out=E[:], in_=E[:], func=AF.Ln, bias=1.0), reads=[tE], writes=[tE])
            yield
            op("act", lambda e: e.activation(out=E[:], in_=E[:], func=AF.Exp, scale=-1.0), reads=[tE], writes=[tE])
            yield
            op("pool", lambda e: e.tensor_tensor(out=sgl[:, ts], in0=sgl[:, ts], in1=E[:], op=ALU.mult), reads=[tSGL, tE], writes=[tSGL])
            yield
        for t0 in range(0, TT, 2):
            gens = [tile(tt) for tt in range(t0, min(TT, t0 + 2))]
            while gens:
                for g in list(gens):
                    try:
                        next(g)
                    except StopIteration:
                        gens.remove(g)
                        continue
                    yield

    def lru_chain(gi, l, n):
        P, tP = ppt[gi % 2], pp_tok[gi % 2]
        Dr, tD = der[gi % 2], der_tok[gi % 2]
        ch = PPL["chan"]
        c0 = 8
        for jj in range(4):
            op("dve", lambda e, jj=jj: e.tensor_scalar(out=dgw[:, jj, :], in0=ident_f[:], scalar1=P[:, ch + jj * NC + n:ch + jj * NC + n + 1],
                                                       scalar2=None, op0=ALU.mult), reads=[tP, tC], writes=[tDG])
        yield
        def lru_tile(tt):
            ts = slice(tt * 512, (tt + 1) * 512)
            ps, tps = psr.next()

            def cv(e, tt=tt, ps=ps):
                inst = None
                for jj in range(4):
                    inst = e.matmul(ps[:, :], lhsT=dgw[:, jj, :], rhs=xl[:, 1 + jj + tt * 512:1 + jj + (tt + 1) * 512],
                                    start=(jj == 0), stop=(jj == 3))
                return inst
            op("pe", cv, reads=[tDG, tXL], writes=[tps])
            xcf, tXCF = lru_r.next()
            xcb, tXCB = xcbr.next()
            op("act", lambda e, ps=ps, xcf=xcf: e.activation(out=xcf[:], in_=ps[:, :], func=AF.Identity,
                                                             bias=P[:, ch + 4 * NC + n:ch + 4 * NC + n + 1]),
               reads=[tps, tP], writes=[tXCF])
            yield
            op("dve", lambda e, xcf=xcf, xcb=xcb: e.tensor_copy(out=xcb[:], in_=xcf[:]), reads=[tXCF], writes=[tXCB])
            yield
            yield
            pr, tpr = psr.next()
            op("pe", lambda e, pr=pr, xcb=xcb: e.matmul(pr[:, :], lhsT=wg[:, n * 128:(n + 1) * 128], rhs=xcb[:], start=True, stop=True),
               reads=[wg_tok, tXCB], writes=[tpr])
            pi, tpi = psr.next()
            op("pe", lambda e, pi=pi, xcb=xcb: e.matmul(pi[:, :], lhsT=wg[:, (NC + n) * 128:(NC + n + 1) * 128], rhs=xcb[:],
                                                        start=True, stop=True), reads=[wg_tok, tXCB], writes=[tpi])
            t1, tt1 = lru_r.next()
            t2, tt2 = lru_r.next()
            t3, tt3 = lru_r.next()
            a, ta = t1, tt1
            nbr = Dr[:, c0 + n:c0 + n + 1]
            nbi = Dr[:, c0 + NC + n:c0 + NC + n + 1]
            cneg = Dr[:, c0 + 2 * NC + n:c0 + 2 * NC + n + 1]
            op("act", lambda e, pr=pr: e.activation(out=t1[:], in_=pr[:, :], func=AF.Exp, scale=-1.0, bias=nbr), reads=[tpr, tD], writes=[tt1])
            op("act", lambda e, pi=pi: e.activation(out=t3[:], in_=pi[:, :], func=AF.Exp, scale=-1.0, bias=nbi), reads=[tpi, tD], writes=[tt3])
            yield
            op("act", lambda e: e.activation(out=t1[:], in_=t1[:], func=AF.Ln, bias=1.0), reads=[tt1], writes=[tt1])
            op("act", lambda e: e.activation(out=t3[:], in_=t3[:], func=AF.Ln, bias=1.0), reads=[tt3], writes=[tt3])
            yield
            op("act", lambda e: e.activation(out=t1[:], in_=t1[:], func=AF.Exp, scale=-1.0), reads=[tt1], writes=[tt1])
            yield
            op("act", lambda e: e.activation(out=a[:], in_=t1[:], func=AF.Exp, scale=cneg), reads=[tt1, tD], writes=[ta])
            yield
            op("pool", lambda e: e.tensor_tensor(out=t2[:], in0=a[:], in1=a[:], op=ALU.mult), reads=[ta], writes=[tt2])
            yield
            op("act", lambda e: e.activation(out=t2[:], in_=t2[:], func=AF.Ln, scale=-1.0, bias=1.0000001), reads=[tt2], writes=[tt2])
            yield
            op("dve", lambda e: e.scalar_tensor_tensor(out=t2[:], in0=t2[:], scalar=0.5, in1=t3[:], op0=ALU.mult, op1=ALU.subtract),
               reads=[tt2, tt3], writes=[tt2])
            yield
            op("act", lambda e: e.activation(out=t2[:], in_=t2[:], func=AF.Exp), reads=[tt2], writes=[tt2])
            yield
            op("pool", lambda e, xcf=xcf: e.tensor_tensor(out=t2[:], in0=t2[:], in1=xcf[:], op=ALU.mult), reads=[tt2, tXCF], writes=[tt2])
            yield
            hs, tHS = t3, tt3
            init = 0.0 if tt == 0 else hcar[:, (tt - 1) % 4:(tt - 1) % 4 + 1]
            op("dve", lambda e, hs=hs, init=init: e.tensor_tensor_scan(out=hs[:], data0=a[:], data1=t2[:], initial=init,
                                                                       op0=ALU.mult, op1=ALU.add), reads=[ta, tt2, tHC], writes=[tHS])
            op("dve", lambda e, hs=hs, tt=tt: e.tensor_copy(out=hcar[:, tt % 4:tt % 4 + 1], in_=hs[:, 511:512]), reads=[tHS], writes=[tHC])
            op("pool", lambda e, ts=ts, hs=hs: e.tensor_tensor(out=yT[:, NH + n, ts], in0=hs[:], in1=sgl[:, ts], op=ALU.mult),
               reads=[tHS, tSGL], writes=[yT_tok[NH + n]])

        for t0 in range(0, TT, 2):
            gens = [lru_tile(tt) for tt in range(t0, min(TT, t0 + 2))]
            while gens:
                for g in list(gens):
                    try:
                        next(g)
                    except StopIteration:
                        gens.remove(g)
                        continue
                    yield

    DW = min(512, D)
    NHALF = D // DW
    xs_tok = [[Tok() for _ in range(TB)] for _ in range(NSEQ)]

    def boundary_block(gi, s, l, b):
        par = b % BW
        pss_ = [bring.next() for _ in range(NHALF)]

        def mmo(e):
            inst = None
            for hf in range(NHALF):
                for fc in range(FC):
                    inst = e.matmul(pss_[hf][0][:, 0:DW], lhsT=yT[:, fc, b * 128:(b + 1) * 128], rhs=wout[:, fc, hf * DW:(hf + 1) * DW],
                                    start=(fc == 0), stop=(fc == FC - 1))
            return inst
        op("pe", mmo, reads=yT_tok + wout_tok, writes=[t for _, t in pss_])
        xt, txt = xt_v[par]
        src = x_d if l == 0 else xs_d
        dst = out_d if l == L - 1 else xs_d
        rd = [xs_tok[s][b]] if l > 0 else []
        Sc.dma("sp", lambda e: e.dma_start(out=xt, in_=src[s, b * 128:(b + 1) * 128, :]), reads=rd, writes=txt)
        yield
        col, tcol = colr.next()
        junk, tj = jk_v[par]
        for hf in range(NHALF):
            op("act", lambda e, hf=hf: e.activation(out=junk[:, hf * DW:(hf + 1) * DW], in_=pss_[hf][0][:, 0:DW], func=AF.Square,
                                                    accum_out=col[:, hf:hf + 1]), reads=[pss_[hf][1]], writes=tj + [tcol])
        yield
        if NHALF == 2:
            op("dve", lambda e: e.tensor_tensor(out=col[:, 0:1], in0=col[:, 0:1], in1=col[:, 1:2], op=ALU.add), reads=[tcol], writes=[tcol])
            yield
        op("act", lambda e: e.activation(out=col[:, 2:3], in_=col[:, 0:1], func=AF.Ln, scale=1.0 / D, bias=NORM_EPS),
           reads=[tcol], writes=[tcol])
        yield
        op("act", lambda e: e.activation(out=col[:, 3:4], in_=col[:, 2:3], func=AF.Exp, scale=-0.5), reads=[tcol], writes=[tcol])
        yield
        tm, ttm = tm_v[par]
        for hf in range(NHALF):
            op("dve", lambda e, hf=hf: e.scalar_tensor_tensor(out=tm[:, hf * DW:(hf + 1) * DW], in0=pss_[hf][0][:, 0:DW], scalar=col[:, 3:4],
                                                              in1=gpost[:, hf * DW:(hf + 1) * DW], op0=ALU.mult, op1=ALU.mult),
               reads=[pss_[hf][1], tcol, tGPOST], writes=ttm)
        yield
        op("dve", lambda e: e.tensor_tensor(out=xt, in0=xt, in1=tm, op=ALU.add), reads=txt + ttm, writes=txt)
        yield
        Sc.dma("sp", lambda e: e.dma_start(out=dst[s, b * 128:(b + 1) * 128, :], in_=xt), reads=txt,
               writes=([xs_tok[s][b]] if l < L - 1 else []))
        if l < L - 1:
            yield from prenorm_block(gi + 1, xt, txt, b, par)

    def interleave(gens, width=2, stagger=0):
        gens = list(gens)
        active = []
        while gens or active:
            if gens and len(active) < width and (not active or active[-1][1] >= stagger):
                active.append([gens.pop(0), 0])
            for it in list(active):
                try:
                    next(it[0])
                    it[1] += 1
                except StopIteration:
                    active.remove(it)

    assert NH == NC
    pairs = [(s_, l_, i_) for s_ in range(NSEQ) for l_ in range(L) for i_ in range(NH)]
    pair_slabs = {}

    def load_pair(k):
        if k >= len(pairs):
            return
        _, l_, i_ = pairs[k]
        lo = 4 + 2 * (k % 2)
        pair_slabs[k] = ([load_slab(l_, i_, 0), load_slab(l_, NH + i_, 1), load_slab(l_, 2 * NH + i_, 2), load_slab(l_, 3 * NH + i_, 3)],
                         [load_slab(l_, 4 * NH + i_, lo), load_slab(l_, 4 * NH + NC + i_, lo + 1)])

    load_pair(0)
    k = 0
    for s in range(NSEQ):
        gi0 = s * L
        load_layer_params(gi0, 0)

        def first_pre(b, s=s):
            xt, txt = xt_v[b % BW]
            Sc.dma("sp", lambda e: e.dma_start(out=xt, in_=x_d[s, b * 128:(b + 1) * 128, :]), writes=txt)
            yield
            yield from prenorm_block(gi0, xt, txt, b, b % BW)
        interleave([first_pre(b) for b in range(TB)], width=BW, stagger=2)
        for l in range(L):
            gi = gi0 + l
            layer_start_region()
            load_wg(l)
            load_gpost(l)
            if l + 1 < L:
                load_layer_params(gi + 1, l + 1)
            for i in range(NH):
                hsl, lsl = pair_slabs.pop(k)
                if i == 0:
                    run_all(lru_proj_raw(lsl))

                def side_gen(gi=gi, l=l, k=k, i=i):
                    yield from lru_chain(gi, l, i)
                    if i + 1 < NH:
                        yield from lru_proj_raw(pair_slabs[k + 1][1])
                side = side_gen()
                qn = None
                if i + 1 < NH:
                    qn = (lambda k=k, i=i: q_proj(pair_slabs[k + 1][0][0][0], pair_slabs[k + 1][0][0][1], (i + 1) % 2))
                head_stage(gi, l, i, hsl, side=side, prefetch=(lambda k=k: load_pair(k + 1)), qnext=qn, side1=lru_silu())
                k += 1
            flushB(force=True)
            load_wout(l)
            interleave([boundary_block(gi, s, l, b) for b in range(TB)], width=BW, stagger=4)

    if dbg:
        pass
    with ExitStack() as stack:
        Sc.emit(nc, stack)
    return nc


def pack_weights(inp, L, D, NH, NC):
    DC = D // 128
    W = NH * 128
    WL = NC * 128
    w_in = np.asarray(inp["w_in"], np.float32)
    ncc = 4 * NH + 2 * NC
    w_in_r = np.ascontiguousarray(w_in.reshape(L, DC, 128, ncc, 128).transpose(0, 3, 2, 1, 4)).reshape(L, ncc, 128, DC * 128)
    w_out = np.asarray(inp["w_out"], np.float32)
    w_out_r = np.ascontiguousarray(w_out.reshape(L, NH + NC, 128, D))
    wr = np.asarray(inp["w_rgate"], np.float32).transpose(0, 2, 1, 3).reshape(L, 128, NC * 128)
    wi = np.asarray(inp["w_igate"], np.float32).transpose(0, 2, 1, 3).reshape(L, 128, NC * 128)
    w_g = np.ascontiguousarray(np.concatenate([wr, wi], axis=2))
    PPL = pp_layout(D)
    pp = np.zeros((L, 128, PPL["_n"]), np.float32)
    pp_gpre = np.asarray(inp["pre_norm_g"], np.float32).reshape(L, DC, 128).transpose(0, 2, 1)
    gpost = np.ascontiguousarray(np.broadcast_to(np.asarray(inp["post_norm_g"], np.float32)[:, None, :], (L, 128, D)))
    lv = PPL["lamv"]
    for i, k in enumerate(("lambda_q1", "lambda_k1", "lambda_q2", "lambda_k2")):
        pp[:, :, lv + 64 * i:lv + 64 * (i + 1)] = np.asarray(inp[k], np.float32)[:, None, :]
    pp[:, :, PPL["subg"]] = np.asarray(inp["subln_g"], np.float32)
    ch = PPL["chan"]
    cw = np.asarray(inp["conv_w"], np.float32)
    chan = [cw[:, 0], cw[:, 1], cw[:, 2], cw[:, 3], inp["conv_b"], inp["b_rgate"], inp["b_igate"], inp["lru_lambda"]]
    for k, v in enumerate(chan):
        v = np.asarray(v, np.float32).reshape(L, NC, 128).transpose(0, 2, 1)
        pp[:, :, ch + k * NC:ch + (k + 1) * NC] = v
    pp[:, :, PPL["gprec"]:PPL["gprec"] + DC] = pp_gpre
    return {"w_in": w_in_r, "w_out": w_out_r, "w_g": w_g, "pp": pp, "gpost": gpost}


_NC_CACHE = {}


def kernel(**inputs):
    L, D, NH, NC, S = 4, 1024, 8, 8, 2048
    x = np.asarray(inputs["x"], np.float32)
    B = x.shape[0]
    n_cores = 8
    nseq = B // n_cores
    wts = pack_weights(inputs, L, D, NH, NC)
    key = (L, nseq, S, D, NH, NC)
    if key not in _NC_CACHE:
        _NC_CACHE[key] = build_program(L=L, NSEQ=nseq, S=S, D=D, NH=NH, NC=NC)
    nc = _NC_CACHE[key]
    in_maps = []
    for c in range(n_cores):
        m = {"x": np.ascontiguousarray(x[c * nseq:(c + 1) * nseq])}
        m.update(wts)
        in_maps.append(m)
    res = run_bass_kernel_spmd(nc, in_maps, core_ids=list(range(n_cores)))
    return np.concatenate([np.asarray(r["out"], np.float32) for r in res.results], axis=0)
```
